# Optimizing a Trainium2 kernel written in Bass

```python
import jax, jax.numpy as jnp
from jax import lax
import numpy as np

D_MODEL = 1024
BATCH = 2
SEQ = 16384
DEPTH = 2

HEAD_DIM = 64
A_CH = D_MODEL // 2
A_CONV = 31
N_Q_HEADS = (D_MODEL // 2) // HEAD_DIM
N_KV_HEADS = 2
GROUP = N_Q_HEADS // N_KV_HEADS
WINDOW = 128
BLOCK = 128
ROPE_THETA = 500000.0
ROPE_DIM = HEAD_DIM // 4
Q_DIM = N_Q_HEADS * HEAD_DIM
KV_DIM = N_KV_HEADS * HEAD_DIM
EVEN_IN = 2 * A_CH + Q_DIM + 2 * KV_DIM
MIX_DIM = A_CH + Q_DIM
SC_DIM = D_MODEL
SC_CONV = 3
D_FF = 2816
FFN_CONV = 3
N_EVEN = (DEPTH + 1) // 2
N_ODD = DEPTH // 2
RMS_EPS = 1e-6
LN_EPS = 1e-5

kernel_name = "hybrid_conformer_swa_shortconv_trunk"


def rms_norm(x, g):
    xf = x.astype(jnp.float32)
    y = xf * lax.rsqrt(jnp.mean(xf * xf, axis=-1, keepdims=True) + RMS_EPS)
    return (y * g.astype(jnp.float32)).astype(x.dtype)


def layer_norm(x, g, b):
    xf = x.astype(jnp.float32)
    mu = jnp.mean(xf, axis=-1, keepdims=True)
    xc = xf - mu
    y = xc * lax.rsqrt(jnp.mean(xc * xc, axis=-1, keepdims=True) + LN_EPS)
    return (y * g.astype(jnp.float32) + b.astype(jnp.float32)).astype(x.dtype)


def causal_dwconv(x, w):
    k, c = w.shape
    return lax.conv_general_dilated(
        x, w[:, None, :].astype(x.dtype), window_strides=(1,), padding=[(k - 1, 0)],
        dimension_numbers=('NWC', 'WIO', 'NWC'), feature_group_count=c)


def partial_rope(x, positions):
    half = ROPE_DIM // 2
    inv_freq = ROPE_THETA ** (-(jnp.arange(half, dtype=jnp.float32) * 2.0 / ROPE_DIM))
    ang = positions.astype(jnp.float32)[..., None] * inv_freq
    cos = jnp.cos(ang)[:, :, None, :]
    sin = jnp.sin(ang)[:, :, None, :]
    xf = x.astype(jnp.float32)
    x1, x2, rest = xf[..., :half], xf[..., half:ROPE_DIM], xf[..., ROPE_DIM:]
    out = jnp.concatenate([x1 * cos - x2 * sin, x2 * cos + x1 * sin, rest], axis=-1)
    return out.astype(x.dtype)


def sliding_window_attention(q, k, v, sinks):
    bsz, s_len = q.shape[0], q.shape[1]
    nb = s_len // BLOCK
    qb = q.reshape(bsz, nb, BLOCK, N_KV_HEADS, GROUP, HEAD_DIM)
    pad = ((0, 0), (BLOCK, 0), (0, 0), (0, 0))
    kb = jnp.pad(k, pad).reshape(bsz, nb + 1, BLOCK, N_KV_HEADS, HEAD_DIM)
    vb = jnp.pad(v, pad).reshape(bsz, nb + 1, BLOCK, N_KV_HEADS, HEAD_DIM)
    kw = jnp.concatenate([kb[:, :-1], kb[:, 1:]], axis=2)
    vw = jnp.concatenate([vb[:, :-1], vb[:, 1:]], axis=2)
    s = jnp.einsum('bnqhgd,bnkhd->bnhgqk', qb, kw,
                   preferred_element_type=jnp.float32) * (HEAD_DIM ** -0.5)
    q_idx = jnp.arange(BLOCK)[:, None]
    k_idx = jnp.arange(2 * BLOCK)[None, :]
    diff = q_idx + BLOCK - k_idx
    band = (diff >= 0) & (diff < WINDOW)
    key_valid = (jnp.arange(nb)[:, None] * BLOCK - BLOCK + k_idx) >= 0
    mask = band[None, :, :] & key_valid[:, None, :]
    s = jnp.where(mask[None, :, None, None], s, -jnp.inf)
    sink = sinks.astype(jnp.float32).reshape(N_KV_HEADS, GROUP)[None, None, :, :, None, None]
    m = jnp.maximum(jnp.max(s, axis=-1, keepdims=True), sink)
    p = jnp.exp(s - m)
    denom = jnp.sum(p, axis=-1, keepdims=True) + jnp.exp(sink - m)
    o = jnp.einsum('bnhgqk,bnkhd->bnqhgd', (p / denom).astype(v.dtype), vw)
    return o.reshape(bsz, s_len, N_Q_HEADS * HEAD_DIM)


def conv_attn_mixer(h, positions, w_in, a_conv_w, a_conv_b, a_ln_g, a_ln_b, sinks, w_out):
    bsz, s_len = h.shape[0], h.shape[1]
    z = h @ w_in
    a_lin, a_gate, q, k, v = jnp.split(
        z, [A_CH, 2 * A_CH, 2 * A_CH + Q_DIM, 2 * A_CH + Q_DIM + KV_DIM], axis=-1)
    a = a_lin * jax.nn.sigmoid(a_gate)
    a = causal_dwconv(a, a_conv_w) + a_conv_b
    a = jax.nn.silu(layer_norm(a, a_ln_g, a_ln_b))
    q = partial_rope(q.reshape(bsz, s_len, N_Q_HEADS, HEAD_DIM), positions)
    k = partial_rope(k.reshape(bsz, s_len, N_KV_HEADS, HEAD_DIM), positions)
    v = v.reshape(bsz, s_len, N_KV_HEADS, HEAD_DIM)
    o = sliding_window_attention(q, k, v, sinks)
    return jnp.concatenate([a, o], axis=-1) @ w_out


def short_conv_mixer(h, w_in, conv_w, w_out):
    z = h @ w_in
    b_gate, c_gate, u = jnp.split(z, 3, axis=-1)
    y = b_gate * causal_dwconv(c_gate * u, conv_w)
    return y @ w_out


def conv_glu_ffn(h, w_up, conv_w, w_down):
    u = causal_dwconv(h @ w_up, conv_w)
    g, val = jnp.split(u, 2, axis=-1)
    return (jax.nn.silu(g) * val) @ w_down


def setup_inputs(seed: int = 0) -> dict:
    key = jax.random.key(seed)
    ks = jax.random.split(key, 20)

    def nrm(k, shape, scale):
        return jax.random.normal(k, shape, jnp.float32) * scale

    def gain(k, shape):
        return 1.0 + 0.05 * jax.random.normal(k, shape, jnp.float32)

    x = nrm(ks[0], (BATCH, SEQ, D_MODEL), 1.0)
    offsets = jax.random.randint(ks[1], (BATCH, 1), 0, 4096, dtype=jnp.int32)
    positions = offsets + jnp.arange(SEQ, dtype=jnp.int32)[None, :]
    return {
        'x': x,
        'positions': positions,
        'mix_norm_pre': gain(ks[2], (DEPTH, D_MODEL)),
        'mix_norm_post': gain(ks[3], (DEPTH, D_MODEL)),
        'ffn_norm_pre': gain(ks[4], (DEPTH, D_MODEL)),
        'ffn_norm_post': gain(ks[5], (DEPTH, D_MODEL)),
        'ev_w_in': nrm(ks[6], (N_EVEN, D_MODEL, EVEN_IN), D_MODEL ** -0.5),
        'ev_a_conv_w': nrm(ks[7], (N_EVEN, A_CONV, A_CH), A_CONV ** -0.5),
        'ev_a_conv_b': nrm(ks[8], (N_EVEN, A_CH), 0.02),
        'ev_a_ln_g': gain(ks[9], (N_EVEN, A_CH)),
        'ev_a_ln_b': nrm(ks[10], (N_EVEN, A_CH), 0.02),
        'ev_sinks': nrm(ks[11], (N_EVEN, N_Q_HEADS), 1.0),
        'ev_w_out': nrm(ks[12], (N_EVEN, MIX_DIM, D_MODEL), MIX_DIM ** -0.5),
        'od_w_in': nrm(ks[13], (N_ODD, D_MODEL, 3 * SC_DIM), D_MODEL ** -0.5),
        'od_conv_w': nrm(ks[14], (N_ODD, SC_CONV, SC_DIM), SC_CONV ** -0.5),
        'od_w_out': nrm(ks[15], (N_ODD, SC_DIM, D_MODEL), SC_DIM ** -0.5),
        'ffn_w_up': nrm(ks[16], (DEPTH, D_MODEL, 2 * D_FF), D_MODEL ** -0.5),
        'ffn_conv_w': nrm(ks[17], (DEPTH, FFN_CONV, 2 * D_FF), FFN_CONV ** -0.5),
        'ffn_w_down': nrm(ks[18], (DEPTH, D_FF, D_MODEL), D_FF ** -0.5),
    }


def reference(x, positions, mix_norm_pre, mix_norm_post, ffn_norm_pre, ffn_norm_post,
              ev_w_in, ev_a_conv_w, ev_a_conv_b, ev_a_ln_g, ev_a_ln_b, ev_sinks, ev_w_out,
              od_w_in, od_conv_w, od_w_out, ffn_w_up, ffn_conv_w, ffn_w_down):
    for i in range(DEPTH):
        j = i // 2
        h = rms_norm(x, mix_norm_pre[i])
        if i % 2 == 0:
            m = conv_attn_mixer(h, positions, ev_w_in[j], ev_a_conv_w[j], ev_a_conv_b[j],
                                ev_a_ln_g[j], ev_a_ln_b[j], ev_sinks[j], ev_w_out[j])
        else:
            m = short_conv_mixer(h, od_w_in[j], od_conv_w[j], od_w_out[j])
        x = x + rms_norm(m, mix_norm_post[i])
        h = rms_norm(x, ffn_norm_pre[i])
        f = conv_glu_ffn(h, ffn_w_up[i], ffn_conv_w[i], ffn_w_down[i])
        x = x + rms_norm(f, ffn_norm_post[i])
    return x
```

```python
import numpy as np
from contextlib import ExitStack
import concourse.bass as bass
import concourse.mybir as mybir
from concourse.bass_utils import run_bass_kernel_spmd

F32 = mybir.dt.float32
BF16 = mybir.dt.bfloat16
I32 = mybir.dt.int32
ALU = mybir.AluOpType
AF = mybir.ActivationFunctionType

NCORES = 8
D = 1024
SEQ = 16384
TOK = 4096
HALO = 256
T = TOK + HALO
DFF = 2816
NFF = 2 * DFF // 128
WIN_EXT = 2688
NEG = -30000.0
TWO_PI = 6.283185307179586
C1 = 6.28125
C2 = TWO_PI - C1

SC_GAIN = 0
SC_C31 = 64
SC_CB = 188
SC_LNG = 192
SC_LNB = 196
SC_SINK = 200
SC_ODC = 204
SC_FFC = 228
SC_FS = 492
SC_HM = 493
SC_EPS_RMS = 494
SC_EPS_LN = 495
SC_HALFPI = 496
SC_N = 500


class Buf:
    __slots__ = ("name", "w", "r", "serial")

    def __init__(self, name, serial=False):
        self.name = name
        self.w = None
        self.r = []
        self.serial = serial


class Op:
    __slots__ = ("eng", "fn", "deps", "ms", "is_dma", "dsem", "dcount", "used")


class Sched:
    ENGS = ("pe", "act", "dve", "pool", "sp")

    def __init__(self, n_dma_sems=12):
        self.q = {e: [] for e in self.ENGS}
        self.n_dma_sems = n_dma_sems
        self.dma_rr = {e: 0 for e in self.ENGS}
        self.dma_last = {}

    def add(self, eng, fn, reads=(), writes=(), dma=False, extra=()):
        op = Op()
        op.eng = eng
        op.fn = fn
        op.is_dma = dma
        op.ms = 0
        op.used = False
        op.dsem = None
        op.dcount = 0
        deps = list(extra)
        ser = [b for b in reads if b.serial]
        if ser:
            reads = [b for b in reads if not b.serial]
            writes = list(writes) + ser
        for b in reads:
            if b.w is not None:
                deps.append(b.w)
        for b in writes:
            if b.w is not None:
                deps.append(b.w)
            deps.extend(b.r)
        if dma:
            slot = self.dma_rr[eng]
            self.dma_rr[eng] = (slot + 1) % self.n_dma_sems
            prev = self.dma_last.get((eng, slot))
            if prev is not None:
                deps.append(prev)
                op.dcount = prev.dcount + 16
            else:
                op.dcount = 16
            op.dsem = (eng, slot)
            self.dma_last[(eng, slot)] = op
        for b in reads:
            b.r.append(op)
        for b in writes:
            b.w = op
            b.r = []
        op.deps = [d for d in set(deps)
                   if d is not op and not (eng == "pe" and d.eng == "pe" and not d.is_dma and not dma)]
        for d in op.deps:
            d.used = True
        self.q[eng].append(op)
        return op

    def emit(self, nc):
        for e in self.ENGS:
            n = 0
            for op in self.q[e]:
                if op.used and not op.is_dma:
                    n += 1
                    op.ms = n
            assert n < 60000, (e, n)
        with ExitStack() as es:
            csem = {e: es.enter_context(nc.semaphore("c_" + e)) for e in ("pe", "act", "dve", "pool")}
            dsem = {}
            for key in self.dma_last:
                dsem[key] = es.enter_context(nc.semaphore("d_%s%d" % key))
            block = es.enter_context(nc.Block())
            q = self.q
            dma_last = self.dma_last

            def run(e, eng):
                waited = {}
                for op in q[e]:
                    need = {}
                    for d in op.deps:
                        if d.is_dma:
                            k, v = ("d",) + d.dsem, d.dcount
                        else:
                            k, v = ("c", d.eng), d.ms
                        if need.get(k, 0) < v:
                            need[k] = v
                    for k, v in need.items():
                        if waited.get(k, 0) >= v:
                            continue
                        waited[k] = v
                        s = csem[k[1]] if k[0] == "c" else dsem[(k[1], k[2])]
                        eng.wait_ge(s, v)
                    ins = op.fn(eng)
                    if op.is_dma:
                        ins.then_inc(dsem[op.dsem], 16)
                    elif op.used:
                        ins.then_inc(csem[e], 1)
                if e == "sp":
                    for key, op in dma_last.items():
                        if waited.get(("d",) + key, 0) < op.dcount:
                            eng.wait_ge(dsem[key], op.dcount)

            @block.tensor
            def _(eng):
                run("pe", eng)

            @block.scalar
            def _(eng):
                run("act", eng)

            @block.vector
            def _(eng):
                run("dve", eng)

            @block.gpsimd
            def _(eng):
                run("pool", eng)

            @block.sync
            def _(eng):
                run("sp", eng)


class Rot:
    mk = Buf

    def __init__(self, alloc, name, k, shape, dt):
        self.items = [(alloc(name + str(i), shape, dt), Rot.mk(name + str(i))) for i in range(k)]
        self.i = 0

    def next(self):
        it = self.items[self.i]
        self.i = (self.i + 1) % len(self.items)
        return it


class _Stop(Exception):
    pass


def build_program(debug=False, nphase=4, maxtiles=None, stage=99):
    nc = bass.Bass("TRN2", target_bir_lowering=False)
    S = Sched()

    def din(name, shape, dt=F32):
        return nc.dram_tensor(name, list(shape), dt, kind="ExternalInput").ap()

    xT = din("xT", [D, T])
    posr = din("posr", [128, T], I32)
    small_d = din("small", [128, SC_N])
    cst_d = din("cst", [128, 128 + 2 * 512])
    w1in_d = din("w1in", [D, WIN_EXT])
    w1out_d = din("w1out", [D, D])
    w3in_d = din("w3in", [D, 3 * D])
    w3out_d = din("w3out", [D, D])
    wup_d = din("wup", [2, D, 2 * DFF])
    wdn_d = din("wdn", [2, DFF, D])
    outT = nc.dram_tensor("outT", [D, TOK], F32, kind="ExternalOutput").ap()
    kind = {"kind": "ExternalOutput"} if debug else {}
    xsA = nc.dram_tensor("xsA", [D, T], F32, **kind).ap()
    xsB = nc.dram_tensor("xsB", [D, T], F32, **kind).ap()

    def tview(ap):
        return ap.rearrange("(c p) t -> p c t", p=128)

    xT_v, xsA_v, xsB_v, outT_v = tview(xT), tview(xsA), tview(xsB), tview(outT)
    dbufs = {"xsA": [Buf("xsA%d" % i) for i in range(T // 256 + 1)],
             "xsB": [Buf("xsB%d" % i) for i in range(T // 256 + 1)]}

    def dblocks(key, s, n):
        return [dbufs[key][i] for i in range(s // 256, (s + n - 1) // 256 + 1)]

    fence = []

    def mkbuf(name):
        b = Buf(name)
        b.r = list(fence)
        return b

    def set_fence():
        fence[:] = []
        for e in ("pe", "act", "dve", "pool"):
            for op in reversed(S.q[e]):
                if not op.is_dma:
                    fence.append(op)
                    break

    def chk(k):
        if stage < k:
            raise _Stop()

    with ExitStack() as es:
        scope = [es]

        def sb(name, shape, dt):
            return scope[0].enter_context(nc.sbuf_tensor(name, list(shape), dt))

        Rot.mk = staticmethod(mkbuf)
        small = sb("smallt", [128, SC_N], F32)
        cst = sb("cstt", [128, 128 + 2 * 512], BF16)
        ones = sb("ones", [128, 128], BF16)
        olo = sb("olo", [128, 128], BF16)
        ohi = sb("ohi", [128, 128], BF16)
        esink = sb("esink", [128, 4], F32)
        b_small, b_cst, b_ones, b_olo, b_ohi, b_esink = (Buf(n) for n in ("small", "cst", "ones", "olo", "ohi", "esink"))

        psum = [es.enter_context(nc.psum_tensor("ps%d" % i, [128, 512], F32)) for i in range(8)]
        pbufs = [Buf("bank%d" % i, serial=True) for i in range(8)]
        pbusy = [False] * 8
        pfreeq = list(range(8))

        def palloc():
            if not pfreeq:
                raise RuntimeError("out of PSUM banks")
            i = pfreeq.pop(0)
            pbusy[i] = True
            return i

        def pfree(i):
            assert pbusy[i]
            pbusy[i] = False
            pfreeq.append(i)

        def sc(col, n=1):
            return small[:, col:col + n]

        def pv3(bk):
            return psum[bk][:, :].rearrange("p (a b) -> p a b", a=2)

        ident = cst[:, 0:128]
        mask_a = cst[:, 128:640]
        mask_af = cst[:, 640:1152]

        def DMA(q, out, in_, reads=(), writes=()):
            return S.add(q, lambda e, o=out, i=in_: e.dma_start(out=o, in_=i), reads, writes, dma=True)

        def MM(out, lhsT, rhs, start, stop, reads, writes, sgc=False):
            if sgc:
                return S.add("pe", lambda e, o=out, l=lhsT, r=rhs, a=start, b=stop:
                             e.matmul(o, lhsT=l, rhs=r, start=a, stop=b, skip_group_check=True), reads, writes)
            return S.add("pe", lambda e, o=out, l=lhsT, r=rhs, a=start, b=stop: e.matmul(o, lhsT=l, rhs=r, start=a, stop=b),
                         reads, writes)

        def ACT(out, in_, func, reads, writes, bias=None, scale=None):
            def fn(e, o=out, i=in_, f=func, b=bias, s=scale):
                kw = {}
                if b is not None:
                    kw["bias"] = b
                if s is not None:
                    kw["scale"] = s
                return e.activation(out=o, in_=i, func=f, **kw)
            return S.add("act", fn, reads, writes)

        def TT(eng, out, in0, in1, op, reads, writes):
            return S.add(eng, lambda e, o=out, a=in0, b=in1, p=op: e.tensor_tensor(out=o, in0=a, in1=b, op=p), reads, writes)

        def TS(eng, out, in0, s1, op0, reads, writes, s2=None, op1=None):
            def fn(e, o=out, a=in0, x1=s1, x2=s2, p0=op0, p1=op1):
                if p1 is None:
                    return e.tensor_scalar(out=o, in0=a, scalar1=x1, scalar2=None, op0=p0)
                return e.tensor_scalar(out=o, in0=a, scalar1=x1, scalar2=x2, op0=p0, op1=p1)
            return S.add(eng, fn, reads, writes)

        def STT(out, in0, scalar, in1, op0, op1, reads, writes):
            return S.add("dve", lambda e, o=out, a=in0, s=scalar, b=in1, p0=op0, p1=op1:
                         e.scalar_tensor_tensor(out=o, in0=a, scalar=s, in1=b, op0=p0, op1=p1), reads, writes)

        def COPY(eng, out, in_, reads, writes):
            if eng == "act":
                return S.add(eng, lambda e, o=out, i=in_: e.activation(out=o, in_=i, func=AF.Copy), reads, writes)
            return S.add(eng, lambda e, o=out, i=in_: e.tensor_copy(out=o, in_=i), reads, writes)

        def RECIP(out, in_, reads, writes):
            return S.add("dve", lambda e, o=out, i=in_: e.reciprocal(out=o, in_=i), reads, writes)

        def MEMSET(eng, ap, val, writes):
            return S.add(eng, lambda e, a=ap, v=val: e.memset(a, v), (), writes)

        DMA("sp", small[:], small_d, writes=[b_small])
        DMA("pool", cst[:], cst_d, writes=[b_cst])
        MEMSET("pool", ones[:], 1.0, [b_ones])
        MEMSET("pool", olo[:], 0.0, [b_olo])
        MEMSET("pool", olo[:, 0:64], 1.0, [b_olo])
        MEMSET("pool", ohi[:], 0.0, [b_ohi])
        MEMSET("pool", ohi[:, 64:128], 1.0, [b_ohi])
        ACT(esink[:], sc(SC_SINK, 4), AF.Exp, [b_small], [b_esink])

        NT = 256
        xp_rot = Rot(sb, "xp", 1, [128, 8, NT], F32)
        xr_rot = Rot(sb, "xr", 1, [128, 8, NT], F32)
        sq_rot = Rot(sb, "sq", 2, [128, 8, NT], BF16)
        h_rot = Rot(sb, "h", 2, [128, 8, NT], BF16)
        stat_rot = Rot(sb, "stat", 3, [128, NT], F32)
        tmp_rot = Rot(sb, "tmp", 2, [128, NT], F32)

        def load_weight_rows(WA, dram2d, col0, nchunks, ncols, tag):
            bufs = []
            for c in range(nchunks):
                b = mkbuf("%s%d" % (tag, c))
                DMA("pool", WA[:, col0 + c * ncols: col0 + (c + 1) * ncols], dram2d[c * 128:(c + 1) * 128, :],
                    writes=[b])
                bufs.append(b)
            return bufs

        def rms_rstd(sq_chunks, n, reads):
            bk = palloc()
            nchk = len(sq_chunks)
            for c, ap in enumerate(sq_chunks):
                MM(psum[bk][:, 0:n], ones[:], ap, c == 0, c == nchk - 1, list(reads) + [b_ones], [pbufs[bk]])
            sd, sdb = stat_rot.next()
            ACT(sd[:, 0:n], psum[bk][:, 0:n], AF.Sqrt, [pbufs[bk], b_small], [sdb], bias=sc(SC_EPS_RMS), scale=1.0 / D)
            pfree(bk)
            rs, rsb = stat_rot.next()
            RECIP(rs[:, 0:n], sd[:, 0:n], [sdb], [rsb])
            return rs, rsb

        def load_x(src_v, src_key, s, n, pool):
            xt, xb = pool.next()
            rd = dblocks(src_key, s, n) if src_key else []
            DMA("sp", xt[:, :, 0:n], src_v[:, :, s:s + n], reads=rd, writes=[xb])
            return xt, xb

        def prenorm_a(xt, xb, n):
            sq, sqb = sq_rot.next()
            ACT(sq[:, :, 0:n], xt[:, :, 0:n], AF.Square, [xb], [sqb])
            return sq, sqb

        def prenorm_b(xt, xb, sq, sqb, n, gcol):
            rs, rsb = rms_rstd([sq[:, c, 0:n] for c in range(8)], n, [sqb])
            h, hb = h_rot.next()
            for c in range(8):
                STT(h[:, c, 0:n], xt[:, c, 0:n], sc(gcol + c), rs[:, 0:n], ALU.mult, ALU.mult, [xb, rsb, b_small], [hb])
            return h, hb

        def out_proj_slices(kchunks, wfn, n, c0, order=None):
            w = n - c0
            nk = len(kchunks)
            banks = []

            order = list(range(nk)) if order is None else order

            def mk(pos, k):
                def fn():
                    if pos == 0:
                        for mp in range(4):
                            banks.append(palloc())
                    ap, kb, wb = kchunks[k]
                    for mp in range(4):
                        bk = banks[mp]
                        for hf in range(2):
                            MM(psum[bk][:, hf * 256:hf * 256 + w], wfn(k, 2 * mp + hf), ap, pos == 0 and hf == 0, pos == nk - 1,
                               [kb, wb], [pbufs[bk]], sgc=True)
                return fn
            return banks, [mk(pos, k) for pos, k in enumerate(order)]

        def norm_residual(banks, xt, xb, n, c0, gcol, s):
            w = n - c0
            sq, sqb = norm_residual_a(banks, n, c0)
            norm_residual_b(banks, sq, sqb, xt, xb, n, c0, gcol, s)

        def norm_residual_a(banks, n, c0):
            w = n - c0
            sq, sqb = sq_rot.next()
            for mp, bk in enumerate(banks):
                ACT(sq[:, 2 * mp:2 * mp + 2, 0:w], pv3(bk)[:, :, 0:w], AF.Square, [pbufs[bk]], [sqb])
            return sq, sqb

        def norm_residual_b(banks, sq, sqb, xt, xb, n, c0, gcol, s):
            w = n - c0
            rs, rsb = rms_rstd([sq[:, c, 0:w] for c in range(8)], w, [sqb])
            for mp, bk in enumerate(banks):
                for hf in range(2):
                    c = 2 * mp + hf
                    tm, tmb = tmp_rot.next()
                    STT(tm[:, 0:w], psum[bk][:, hf * 256:hf * 256 + w], sc(gcol + c), rs[:, 0:w], ALU.mult, ALU.mult,
                        [pbufs[bk], rsb, b_small], [tmb])
                    TT("dve", xt[:, c, c0:n], tm[:, 0:w], xt[:, c, c0:n], ALU.add, [tmb, xb], [xb])
                pfree(bk)
            if s < HALO:
                hc = min(HALO - s, n)
                TS("dve", xt[:, :, 0:hc], xt[:, :, 0:hc], sc(SC_HM), ALU.mult, [xb, b_small], [xb])

        def out_proj_norm_residual(kchunks, wfn, xt, xb, n, c0, gcol, s):
            banks, sl = out_proj_slices(kchunks, wfn, n, c0)
            for f in sl:
                f()
            norm_residual(banks, xt, xb, n, c0, gcol, s)

        def store_x(xt, xb, dst_v, dst_key, s, n, wlo, dst_off=0):
            wr = dblocks(dst_key, wlo, s + n - wlo) if dst_key else []
            DMA("sp", dst_v[:, :, wlo - dst_off:s + n - dst_off], xt[:, :, wlo - s:n], reads=[xb], writes=wr)

        def run_pipeline(nt, prep, A1, A2, Bst):
            if nt == 0:
                return
            st = {0: prep(0)}
            A1(0, st[0])
            if nt > 1:
                st[1] = prep(1)
            A2(0, st[0])
            for i in range(nt):
                if i + 1 < nt:
                    A1(i + 1, st[i + 1])
                if i + 2 < nt:
                    st[i + 2] = prep(i + 2)
                Bst(i, st[i])
                del st[i]
                if i + 1 < nt:
                    A2(i + 1, st[i + 1])

        def run_pipeline2(nt, prep_a, prep_b, nunits, unit, unit_end, bslices, bfinal_a, bfinal_b, sched, pa_at, pb_at, fa_at):
            if nt == 0:
                return
            st = {0: prep_b(prep_a(0))}
            for j in range(nunits):
                unit(0, st[0], j)
                if j == pa_at and nt > 1:
                    st[1] = prep_a(1)
                if j == pb_at and nt > 1:
                    st[1] = prep_b(st[1])
            unit_end(0, st[0])
            for i in range(nt):
                pending = bslices(i, st[i])
                idx = 0
                if i + 1 < nt:
                    for j in range(nunits):
                        unit(i + 1, st[i + 1], j)
                        for _ in range(sched.get(j, 0)):
                            if idx < len(pending):
                                pending[idx]()
                                idx += 1
                        if j == fa_at:
                            while idx < len(pending):
                                pending[idx]()
                                idx += 1
                            bfinal_a(i, st[i])
                        if j == pa_at and i + 2 < nt:
                            st[i + 2] = prep_a(i + 2)
                        if j == pb_at and i + 2 < nt:
                            st[i + 2] = prep_b(st[i + 2])
                    unit_end(i + 1, st[i + 1])
                else:
                    while idx < len(pending):
                        pending[idx]()
                        idx += 1
                    bfinal_a(i, st[i])
                bfinal_b(i, st[i])
                del st[i]

        def run_pipeline3(nt, prep_a, prep_b, A1, A2, bslices, bfinal_a, bfinal_b, prep_tick=0):
            if nt == 0:
                return

            def noop():
                pass

            st = {0: prep_b(prep_a(0))}
            A1(0, st[0], noop)
            if nt > 1:
                st[1] = prep_b(prep_a(1))
            A2(0, st[0], noop)
            for i in range(nt):
                pending = bslices(i, st[i])
                state = {'idx': 0, 'calls': 0, 'prepped': False}

                def do_prep(i=i, state=state):
                    if not state['prepped'] and i + 2 < nt:
                        st[i + 2] = prep_a(i + 2)
                    state['prepped'] = True

                def tick(pending=pending, state=state, do_prep=do_prep):
                    if state['calls'] == prep_tick:
                        do_prep()
                    state['calls'] += 1
                    if state['idx'] < len(pending):
                        pending[state['idx']]()
                        state['idx'] += 1

                if prep_tick == 0:
                    do_prep()
                if i + 1 < nt:
                    A1(i + 1, st[i + 1], tick)
                do_prep()
                while state['idx'] < len(pending):
                    pending[state['idx']]()
                    state['idx'] += 1
                bfinal_a(i, st[i])

                def mid(i=i):
                    if i + 2 < nt:
                        st[i + 2] = prep_b(st[i + 2])
                    bfinal_b(i, st[i])

                if i + 1 < nt:
                    A2(i + 1, st[i + 1], mid)
                else:
                    mid()
                del st[i]

        ph1 = es.enter_context(ExitStack())
        scope[0] = ph1
        W1IN, W1OUT, DIAG = 0, 8 * WIN_EXT, 8 * WIN_EXT + 8 * D
        WA1 = sb("WA1", [128, DIAG + 124 * 128], BF16)
        WA1v = WA1[:, W1IN:W1IN + 8 * WIN_EXT].rearrange("p (c m) -> p c m", c=8)
        w1in_v = w1in_d.rearrange("(c p) m -> p c m", p=128)
        win_piece = {}
        for nm, (c0, c1) in (("v", (2560, 2688)), ("k", (2048, 2560)), ("q", (1024, 2048)), ("glu", (0, 1024))):
            b = mkbuf("w1in_" + nm)
            for cc0 in range(c0, c1, 512):
                cc1 = min(cc0 + 512, c1)
                DMA("pool", WA1v[:, :, cc0:cc1], w1in_v[:, :, cc0:cc1], writes=[b])
            win_piece[nm] = b

        def winb(off):
            return win_piece["v" if off >= 2560 else "k" if off >= 2048 else "q" if off >= 1024 else "glu"]
        wout_b = load_weight_rows(WA1, w1out_d, W1OUT, 8, D, "w1out")
        b_diag = mkbuf("diag")
        identh = sb("identh", [128, 128], BF16)
        b_identh = mkbuf("identh")
        TS("dve", identh[:], ident, 0.5, ALU.mult, [b_cst], [b_identh])
        diag_state = {'done': False}

        def build_diags():
            if diag_state['done']:
                return
            diag_state['done'] = True
            k = 0
            for c in range(4):
                for j in range(31):
                    col = DIAG + (c * 31 + j) * 128
                    if k % 2 == 0:
                        ACT(WA1[:, col:col + 128], identh[:], AF.Identity, [b_identh, b_small], [b_diag],
                            scale=sc(SC_C31 + c * 31 + j))
                    else:
                        TS("pool", WA1[:, col:col + 128], identh[:], sc(SC_C31 + c * 31 + j), ALU.mult,
                           [b_identh, b_small], [b_diag])
                    k += 1

        def w1in(c, off, m):
            return WA1[:, W1IN + c * WIN_EXT + off: W1IN + c * WIN_EXT + off + m]

        NB = NT // 128
        abuf = [sb("abuf%d" % c, [128, 30 + NT], BF16) for c in range(4)]
        b_abuf = [mkbuf("abuf%d" % c) for c in range(4)]
        krot = [sb("krot%d" % g, [128, 128 + NT], BF16) for g in range(2)]
        b_krot = [mkbuf("krot%d" % g) for g in range(2)]
        vlo = [sb("vlo%d" % g, [128, (NB + 1) * 128], BF16) for g in range(2)]
        vhi = [sb("vhi%d" % g, [128, (NB + 1) * 128], BF16) for g in range(2)]
        b_v = [mkbuf("v%d" % g) for g in range(2)]
        for c in range(4):
            MEMSET("pool", abuf[c][:], 0.0, [b_abuf[c]])
        for g in range(2):
            MEMSET("pool", krot[g][:], 0.0, [b_krot[g]])
            MEMSET("pool", vlo[g][:], 0.0, [b_v[g]])
            MEMSET("pool", vhi[g][:], 0.0, [b_v[g]])
        qrot_rot = Rot(sb, "qrot", 8, [128, NT], BF16)
        pT_rot = Rot(sb, "pT", 8, [128, 512], BF16)
        pos_rot = Rot(sb, "posi", 2, [128, NT], I32)
        ki_rot = Rot(sb, "ki", 2, [128, NT], I32)
        cs_rot = Rot(sb, "cs", 4, [128, NT], F32)
        rt_rot = Rot(sb, "rt", 7, [128, NT], F32)
        tg_rot = Rot(sb, "tg", 2, [128, NT], F32)
        ac_rot = Rot(sb, "ac", 4, [128, NT], F32)
        acb_rot = Rot(sb, "acb", 4, [128, NT], BF16)
        sq2_rot = Rot(sb, "sq2", 4, [128, NT], BF16)
        mix_rot = Rot(sb, "mix", 8, [128, NT], BF16)

        ntiles1 = T // NT if maxtiles is None else (maxtiles + (nphase - 1))
        def prep1(ti):
            s, n = ti * NT, NT
            xt, xb = load_x(xT_v, None, s, n, xp_rot)
            pi_, pib = pos_rot.next()
            DMA("sp", pi_[:, 0:n], posr[:, s:s + n], writes=[pib])
            posf, posfb = rt_rot.next()
            COPY("dve", posf[:, 0:n], pi_[:, 0:n], [pib], [posfb])
            ang, angb = rt_rot.next()
            TS("dve", ang[:, 0:n], posf[:, 0:n], sc(SC_FS), ALU.mult, [posfb, b_small], [angb])
            kq, kqb = rt_rot.next()
            TS("dve", kq[:, 0:n], ang[:, 0:n], 1.0 / TWO_PI, ALU.mult, [angb], [kqb])
            ki, kib = ki_rot.next()
            COPY("dve", ki[:, 0:n], kq[:, 0:n], [kqb], [kib])
            kf, kfb = rt_rot.next()
            COPY("dve", kf[:, 0:n], ki[:, 0:n], [kib], [kfb])
            r1, r1b = rt_rot.next()
            STT(r1[:, 0:n], kf[:, 0:n], -C1, ang[:, 0:n], ALU.mult, ALU.add, [kfb, angb], [r1b])
            r2, r2b = rt_rot.next()
            STT(r2[:, 0:n], kf[:, 0:n], -C2, r1[:, 0:n], ALU.mult, ALU.add, [kfb, r1b], [r2b])
            r3, r3b = rt_rot.next()
            TS("dve", r3[:, 0:n], r2[:, 0:n], -3.1415925, ALU.max, [r2b], [r3b], s2=3.1415925, op1=ALU.min)
            Sf, Sfb = cs_rot.next()
            ACT(Sf[:, 0:n], r3[:, 0:n], AF.Sin, [r3b], [Sfb])
            ar, arb = rt_rot.next()
            ACT(ar[:, 0:n], r3[:, 0:n], AF.Abs, [r3b], [arb])
            Cf, Cfb = cs_rot.next()
            ACT(Cf[:, 0:n], ar[:, 0:n], AF.Sin, [arb, b_small], [Cfb], bias=sc(SC_HALFPI), scale=-1.0)

            sq, sqb = prenorm_a(xt, xb, n)
            return dict(s=s, n=n, xt=xt, xb=xb, sq=sq, sqb=sqb, Cf=Cf, Cfb=Cfb, Sf=Sf, Sfb=Sfb)

        def prep1_b(st):
            st['h'], st['hb'] = prenorm_b(st['xt'], st['xb'], st['sq'], st['sqb'], st['n'], SC_GAIN + (0 * 2 + 0) * 8)
            build_diags()
            return st

        def p1_A1(ti, st, tick):
            s, n, h, hb = st['s'], st['n'], st['h'], st['hb']
            Cf, Cfb, Sf, Sfb = st['Cf'], st['Cfb'], st['Sf'], st['Sfb']
            def proj_into(bk, half, off):
                for c in range(8):
                    MM(psum[bk][:, half * 256:half * 256 + n], w1in(c, off, 128), h[:, c, 0:n], c == 0, c == 7,
                       [hb, winb(off)], [pbufs[bk]])

            bk = palloc()
            for b in range(NB):
                for c in range(8):
                    MM(psum[bk][:, b * 128:(b + 1) * 128], h[:, c, b * 128:(b + 1) * 128], w1in(c, 2560, 128),
                       c == 0, c == 7, [hb, winb(2560)], [pbufs[bk]])
            for b in range(NB):
                for g in range(2):
                    COPY("act", vlo[g][:, (b + 1) * 128:(b + 1) * 128 + 64],
                         psum[bk][:, b * 128 + g * 64:b * 128 + g * 64 + 64], [pbufs[bk]], [b_v[g]])
                    COPY("act", vhi[g][:, (b + 1) * 128 + 64:(b + 2) * 128],
                         psum[bk][:, b * 128 + g * 64:b * 128 + g * 64 + 64], [pbufs[bk]], [b_v[g]])
            pfree(bk)
            tick()

            def rope(off_x, off_r, out_ap, out_buf):
                bk = palloc()
                proj_into(bk, 0, off_x)
                proj_into(bk, 1, off_r)
                t1, t1b = tmp_rot.next()
                TT("dve", t1[:, 0:n], psum[bk][:, 0:n], Cf[:, 0:n], ALU.mult, [pbufs[bk], Cfb], [t1b])
                t2, t2b = tmp_rot.next()
                TT("dve", t2[:, 0:n], psum[bk][:, 256:256 + n], Sf[:, 0:n], ALU.mult, [pbufs[bk], Sfb], [t2b])
                pfree(bk)
                TT("dve", out_ap, t1[:, 0:n], t2[:, 0:n], ALU.add, [t1b, t2b], [out_buf])

            for g in range(2):
                rope(2048 + g * 128, 2304 + g * 128, krot[g][:, 128:128 + n], b_krot[g])
                tick()
            qr_t = []
            for cq in range(4):
                qt, qb = qrot_rot.next()
                rope(1024 + cq * 128, 1536 + cq * 128, qt[:, 0:n], qb)
                qr_t.append((qt, qb))
                tick()
            for c in range(4):
                bk = palloc()
                proj_into(bk, 0, 512 + c * 128)
                proj_into(bk, 1, c * 128)
                tg, tgb = tg_rot.next()
                ACT(tg[:, 0:n], psum[bk][:, 0:n], AF.Tanh, [pbufs[bk]], [tgb], scale=0.5)
                STT(abuf[c][:, 30:30 + n], tg[:, 0:n], 1.0, psum[bk][:, 256:256 + n], ALU.add, ALU.mult,
                    [tgb, pbufs[bk]], [b_abuf[c]])
                pfree(bk)
                tick()
            pTs = {}
            for g in range(2):
                for b in range(NB):
                    gblk = ti * NB + b
                    for half in range(2):
                        bk = palloc()
                        msk = mask_af if gblk == HALO // 128 else mask_a
                        MM(psum[bk][:, :], ident, msk, True, False, [b_cst], [pbufs[bk]])
                        for kb in range(2):
                            kcol = (b + kb) * 128
                            for ii in range(2):
                                qt, qb = qr_t[2 * g + ii]
                                col = (kb * 2 + ii) * 128
                                MM(psum[bk][:, col:col + 128], krot[g][half * 64:(half + 1) * 64, kcol:kcol + 128],
                                   qt[half * 64:(half + 1) * 64, b * 128:(b + 1) * 128], False, kb == 1 and ii == 1,
                                   [b_krot[g], qb], [pbufs[bk]])
                        p_t, p_b = pT_rot.next()
                        ACT(p_t[:, :], psum[bk][:, :], AF.Exp, [pbufs[bk]], [p_b], scale=0.125)
                        pfree(bk)
                        pTs[(g, b, half)] = (p_t, p_b)
                        tick()
            st['pTs'] = pTs
            st['proj_into'] = proj_into

        def p1_A2(ti, st, tick):
            s, n, h, hb = st['s'], st['n'], st['h'], st['hb']
            Cf, Cfb, Sf, Sfb = st['Cf'], st['Cfb'], st['Sf'], st['Sfb']
            pTs, proj_into = st['pTs'], st['proj_into']
            tick()
            acs = []
            for cp in range(2):
                bk = palloc()
                for hf in range(2):
                    c = 2 * cp + hf
                    for j in range(31):
                        col = DIAG + (c * 31 + j) * 128
                        MM(psum[bk][:, hf * 256:hf * 256 + n], WA1[:, col:col + 128], abuf[c][:, j:j + n], j == 0, j == 30,
                           [b_diag, b_abuf[c]], [pbufs[bk]])
                for hf in range(2):
                    c = 2 * cp + hf
                    src = psum[bk][:, hf * 256:hf * 256 + n]
                    ac, acb_ = ac_rot.next()
                    ACT(ac[:, 0:n], src, AF.Identity, [pbufs[bk], b_small], [acb_], bias=sc(SC_CB + c))
                    s2, s2b = sq2_rot.next()
                    ACT(s2[:, 0:n], src, AF.Square, [pbufs[bk], b_small], [s2b], bias=sc(SC_CB + c))
                    a16, a16b = acb_rot.next()
                    COPY("dve", a16[:, 0:n], ac[:, 0:n], [acb_], [a16b])
                    acs.append((ac, acb_, a16, a16b, s2, s2b))
                    COPY("dve", abuf[c][:, 0:30], abuf[c][:, n:n + 30], [b_abuf[c]], [b_abuf[c]])
                pfree(bk)
            ochunks = []
            for cc in range(4):
                g = cc // 2
                iA = 2 * (cc % 2)
                bk = palloc()
                ii = cc % 2
                for b in range(NB):
                    k = 0
                    for kb in range(2):
                        vcol = (b + kb) * 128
                        col = (kb * 2 + ii) * 128
                        for hh, vt in enumerate((vlo[g], vhi[g])):
                            p_t, p_b = pTs[(g, b, hh)]
                            MM(psum[bk][:, b * 128:(b + 1) * 128], vt[:, vcol:vcol + 128],
                               p_t[:, col:col + 128], k == 0, k == 3, [b_v[g], p_b], [pbufs[bk]])
                            k += 1
                for b in range(NB):
                    k = 0
                    for kb in range(2):
                        col = (kb * 2 + ii) * 128
                        for hh, ot in enumerate((olo, ohi)):
                            p_t, p_b = pTs[(g, b, hh)]
                            MM(psum[bk][:, 256 + b * 128:256 + (b + 1) * 128], ot[:],
                               p_t[:, col:col + 128], k == 0, k == 3, [b_olo, b_ohi, p_b], [pbufs[bk]])
                            k += 1
                dn, dnb = stat_rot.next()
                ACT(dn[:, 0:n], psum[bk][:, 256:256 + n], AF.Identity, [pbufs[bk], b_esink], [dnb], bias=esink[:, cc:cc + 1])
                rd_, rdb = stat_rot.next()
                RECIP(rd_[:, 0:n], dn[:, 0:n], [dnb], [rdb])
                mo, mob = mix_rot.next()
                TT("dve", mo[:, 0:n], psum[bk][:, 0:n], rd_[:, 0:n], ALU.mult, [pbufs[bk], rdb], [mob])
                pfree(bk)
                ochunks.append((mo, mob))
            for g in range(2):
                COPY("dve", krot[g][:, 0:128], krot[g][:, n:n + 128], [b_krot[g]], [b_krot[g]])
                COPY("dve", vlo[g][:, 0:128], vlo[g][:, NB * 128:(NB + 1) * 128], [b_v[g]], [b_v[g]])
                COPY("dve", vhi[g][:, 0:128], vhi[g][:, NB * 128:(NB + 1) * 128], [b_v[g]], [b_v[g]])
            bk = palloc()
            for c in range(4):
                MM(psum[bk][:, 0:n], ones[:], acs[c][2][:, 0:n], c == 0, c == 3, [acs[c][3], b_ones], [pbufs[bk]])
            for c in range(4):
                MM(psum[bk][:, 256:256 + n], ones[:], acs[c][4][:, 0:n], c == 0, c == 3, [acs[c][5], b_ones], [pbufs[bk]])
            mean, meanb = stat_rot.next()
            ACT(mean[:, 0:n], psum[bk][:, 0:n], AF.Copy, [pbufs[bk]], [meanb], scale=1.0 / 512)
            msq, msqb = tmp_rot.next()
            ACT(msq[:, 0:n], psum[bk][:, 0:n], AF.Square, [pbufs[bk]], [msqb], scale=1.0 / 512)
            var, varb = tmp_rot.next()
            STT(var[:, 0:n], psum[bk][:, 256:256 + n], 1.0 / 512, msq[:, 0:n], ALU.mult, ALU.subtract, [pbufs[bk], msqb], [varb])
            pfree(bk)
            sd, sdb = tmp_rot.next()
            ACT(sd[:, 0:n], var[:, 0:n], AF.Sqrt, [varb, b_small], [sdb], bias=sc(SC_EPS_LN))
            rl, rlb = stat_rot.next()
            RECIP(rl[:, 0:n], sd[:, 0:n], [sdb], [rlb])
            achunks = []
            for c in range(4):
                ac, acb_ = acs[c][0], acs[c][1]
                t1, t1b = tmp_rot.next()
                TT("dve", t1[:, 0:n], ac[:, 0:n], mean[:, 0:n], ALU.subtract, [acb_, meanb], [t1b])
                t2, t2b = tmp_rot.next()
                STT(t2[:, 0:n], t1[:, 0:n], sc(SC_LNG + c), rl[:, 0:n], ALU.mult, ALU.mult, [t1b, rlb, b_small], [t2b])
                ma, mab = mix_rot.next()
                ACT(ma[:, 0:n], t2[:, 0:n], AF.Silu, [t2b, b_small], [mab], bias=sc(SC_LNB + c))
                achunks.append((ma, mab))
            st['mixk'] = achunks + ochunks

        def p1_bslices(ti, st):
            s, n = st['s'], st['n']
            st['xr'] = load_x(xT_v, None, s, n, xr_rot)
            mixk = st['mixk']
            kch = [(mixk[k][0][:, 0:n], mixk[k][1], wout_b[k]) for k in range(8)]
            banks, sl = out_proj_slices(kch, lambda k, m: WA1[:, W1OUT + k * D + m * 128: W1OUT + k * D + (m + 1) * 128], n, 0,
                                        order=[4, 5, 6, 7, 0, 1, 2, 3])
            st['banks'] = banks
            return sl

        def p1_bfinal_a(ti, st):
            st['nsq'] = norm_residual_a(st['banks'], st['n'], 0)

        def p1_bfinal_b(ti, st):
            s, n = st['s'], st['n']
            xt, xb = st['xr']
            sq, sqb = st['nsq']
            norm_residual_b(st['banks'], sq, sqb, xt, xb, n, 0, SC_GAIN + (1 * 2 + 0) * 8, s)
            store_x(xt, xb, xsA_v, "xsA", s, n, s)

        run_pipeline3(ntiles1, prep1, prep1_b, p1_A1, p1_A2, p1_bslices, p1_bfinal_a, p1_bfinal_b, prep_tick=7)
        ph1.close()
        set_fence()

        def ffn_phase(layer, src_v, src_key, dst_v, dst_key, s0, dst_off):
            ph = es.enter_context(ExitStack())
            scope[0] = ph
            WUP, WDN = 0, 8 * 2 * DFF
            WA = sb("WAF%d" % layer, [128, WDN + 22 * D], BF16)
            WAv = WA[:, WUP:WUP + 8 * 2 * DFF].rearrange("p (c m) -> p c m", c=8)
            wup_v = wup_d[layer].rearrange("(c p) m -> p c m", p=128)
            up_b = []
            for pc in range(11):
                b = mkbuf("wup%d_p%d" % (layer, pc))
                for base in (0, DFF):
                    c0 = base + pc * 256
                    DMA("pool", WAv[:, :, c0:c0 + 256], wup_v[:, :, c0:c0 + 256], writes=[b])
                up_b.append(b)
            dn_b = load_weight_rows(WA, wdn_d[layer], WDN, 22, D, "wdn%d_" % layer)
            ub_rot = Rot(sb, "ub%d_" % layer, 2, [128, 2, NT], F32)
            ya_rot = Rot(sb, "ya%d_" % layer, 3, [128, NT], F32)
            yy_rot = Rot(sb, "yy%d_" % layer, 8, [128, NT], F32)
            sg_rot = Rot(sb, "sg%d_" % layer, 2, [128, NT], F32)
            act_rot = Rot(sb, "actc%d_" % layer, 33, [128, NT], BF16)
            mt = None if maxtiles is None else maxtiles + (nphase - (2 if layer == 0 else 4))
            tiles = []
            s = s0
            while s + 2 < T and (mt is None or len(tiles) < mt):
                tiles.append((s, min(NT, T - s)))
                s += NT - 2

            def prep(i):
                s, n = tiles[i]
                xt, xb = load_x(src_v, src_key, s, n, xp_rot)
                sq, sqb = prenorm_a(xt, xb, n)
                return dict(s=s, n=n, xt=xt, xb=xb, sq=sq, sqb=sqb, acts=[], pend=None)

            def prep_b(st):
                st['h'], st['hb'] = prenorm_b(st['xt'], st['xb'], st['sq'], st['sqb'], st['n'], SC_GAIN + (2 * 2 + layer) * 8)
                return st

            def finish(st):
                n = st['n']
                yg, ygb, yv, yvb = st['pend']
                sg, sgb = sg_rot.next()
                ACT(sg[:, 2:n], yg[:, 2:n], AF.Silu, [ygb], [sgb])
                at, atb = act_rot.next()
                TT("dve", at[:, 2:n], sg[:, 2:n], yv[:, 2:n], ALU.mult, [sgb, yvb], [atb])
                st['acts'].append((at, atb))
                st['pend'] = None

            def pairs(st, j0, j1):
                n, h, hb = st['n'], st['h'], st['hb']
                for j in range(j0, j1):
                    bk = palloc()
                    for hf, ch in enumerate((j, 22 + j)):
                        for c in range(8):
                            MM(psum[bk][:, hf * 256:hf * 256 + n],
                               WA[:, WUP + c * 2 * DFF + ch * 128: WUP + c * 2 * DFF + (ch + 1) * 128],
                               h[:, c, 0:n], c == 0, c == 7, [hb, up_b[j // 2]], [pbufs[bk]])
                    ub, ubb = ub_rot.next()
                    ACT(ub[:, :, 0:n], pv3(bk)[:, :, 0:n], AF.Copy, [pbufs[bk]], [ubb])
                    yas = []
                    for hf, ch in enumerate((j, 22 + j)):
                        wcol = SC_FFC + (layer * 44 + ch) * 3
                        ya, yab = ya_rot.next()
                        ACT(ya[:, 2:n], psum[bk][:, hf * 256 + 2:hf * 256 + n], AF.Identity, [pbufs[bk], b_small], [yab],
                            scale=sc(wcol + 2))
                        yas.append((ya, yab))
                    pfree(bk)
                    ys = []
                    for hf, ch in enumerate((j, 22 + j)):
                        wcol = SC_FFC + (layer * 44 + ch) * 3
                        ya, yab = yas[hf]
                        y1, y1b = yy_rot.next()
                        STT(y1[:, 2:n], ub[:, hf, 1:n - 1], sc(wcol + 1), ya[:, 2:n], ALU.mult, ALU.add, [ubb, yab, b_small], [y1b])
                        y2, y2b = yy_rot.next()
                        STT(y2[:, 2:n], ub[:, hf, 0:n - 2], sc(wcol + 0), y1[:, 2:n], ALU.mult, ALU.add, [ubb, y1b, b_small], [y2b])
                        ys.append((y2, y2b))
                    if st['pend'] is not None:
                        finish(st)
                    st['pend'] = (ys[0][0], ys[0][1], ys[1][0], ys[1][1])

            def f_unit(i, st, j):
                pairs(st, j, j + 1)

            def f_unit_end(i, st):
                finish(st)

            def f_bslices(i, st):
                s, n, acts = st['s'], st['n'], st['acts']
                st['xr'] = load_x(src_v, src_key, s, n, xr_rot)
                kch = [(acts[k][0][:, 2:n], acts[k][1], dn_b[k]) for k in range(22)]
                banks, sl = out_proj_slices(kch, lambda k, m: WA[:, WDN + k * D + m * 128: WDN + k * D + (m + 1) * 128], n, 2)
                st['banks'] = banks
                return sl

            def f_bfinal_a(i, st):
                st['nsq'] = norm_residual_a(st['banks'], st['n'], 2)

            def f_bfinal_b(i, st):
                s, n = st['s'], st['n']
                xt, xb = st['xr']
                sq, sqb = st['nsq']
                norm_residual_b(st['banks'], sq, sqb, xt, xb, n, 2, SC_GAIN + (3 * 2 + layer) * 8, s)
                store_x(xt, xb, dst_v, dst_key, s, n, s + 2, dst_off)

            sched = {j: 1 for j in range(2, 20)}
            for j in (16, 17, 18, 19):
                sched[j] = 2
            run_pipeline2(len(tiles), prep, prep_b, 22, f_unit, f_unit_end, f_bslices, f_bfinal_a, f_bfinal_b, sched, 7, 10, 19)
            ph.close()
            set_fence()

        if nphase >= 2:
            ffn_phase(0, xsA_v, "xsA", xsB_v, "xsB", HALO - 6, 0)

        if nphase >= 3:
            ph3 = es.enter_context(ExitStack())
            scope[0] = ph3
            W3IN, W3OUT = 0, 8 * 3 * D
            WA3 = sb("WA3", [128, W3OUT + 8 * D], BF16)
            w3in_b = load_weight_rows(WA3, w3in_d, W3IN, 8, 3 * D, "w3in")
            w3out_b = load_weight_rows(WA3, w3out_d, W3OUT, 8, D, "w3out")
            usb_rot = Rot(sb, "usb", 6, [128, NT], F32)
            cu_rot = Rot(sb, "cu", 8, [128, NT], F32)
            yy3_rot = Rot(sb, "yy3_", 18, [128, NT], F32)
            yb_rot = Rot(sb, "ybm", 16, [128, NT], BF16)
            mt = None if maxtiles is None else maxtiles + (nphase - 3)
            tiles = []
            s = HALO - 4
            while s + 2 < T and (mt is None or len(tiles) < mt):
                tiles.append((s, min(NT, T - s)))
                s += NT - 2

            def prep3(i):
                s, n = tiles[i]
                xt, xb = load_x(xsB_v, "xsB", s, n, xp_rot)
                sq, sqb = prenorm_a(xt, xb, n)
                return dict(s=s, n=n, xt=xt, xb=xb, sq=sq, sqb=sqb, ybs=[])

            def prep3_b(st):
                st['h'], st['hb'] = prenorm_b(st['xt'], st['xb'], st['sq'], st['sqb'], st['n'], SC_GAIN + (0 * 2 + 1) * 8)
                return st

            def chunks3(st, c0, c1, tick):
                n, h, hb = st['n'], st['h'], st['hb']

                def proj3(bk, half, off):
                    for c in range(8):
                        MM(psum[bk][:, half * 256:half * 256 + n], WA3[:, W3IN + c * 3 * D + off: W3IN + c * 3 * D + off + 128],
                           h[:, c, 0:n], c == 0, c == 7, [hb, w3in_b[c]], [pbufs[bk]])

                for c in range(c0, c1):
                    bk = palloc()
                    proj3(bk, 0, 2 * D + c * 128)
                    proj3(bk, 1, D + c * 128)
                    usb, usbb = usb_rot.next()
                    ACT(usb[:, 0:n], psum[bk][:, 0:n], AF.Copy, [pbufs[bk]], [usbb])
                    cu, cub = cu_rot.next()
                    TT("dve", cu[:, 0:n], psum[bk][:, 256:256 + n], usb[:, 0:n], ALU.mult, [pbufs[bk], usbb], [cub])
                    pfree(bk)
                    tick()
                    bk = palloc()
                    proj3(bk, 0, c * 128)
                    wcol = SC_ODC + c * 3
                    y0, y0b = yy3_rot.next()
                    ACT(y0[:, 2:n], cu[:, 0:n - 2], AF.Identity, [cub, b_small], [y0b], scale=sc(wcol + 0))
                    y1, y1b = yy3_rot.next()
                    STT(y1[:, 2:n], cu[:, 1:n - 1], sc(wcol + 1), y0[:, 2:n], ALU.mult, ALU.add, [cub, y0b, b_small], [y1b])
                    y2, y2b = yy3_rot.next()
                    STT(y2[:, 2:n], cu[:, 2:n], sc(wcol + 2), y1[:, 2:n], ALU.mult, ALU.add, [cub, y1b, b_small], [y2b])
                    yb_, ybb = yb_rot.next()
                    TT("dve", yb_[:, 2:n], psum[bk][:, 2:n], y2[:, 2:n], ALU.mult, [pbufs[bk], y2b], [ybb])
                    pfree(bk)
                    st['ybs'].append((yb_, ybb))
                    tick()

            def p3_bslices(i, st):
                s, n, ybs = st['s'], st['n'], st['ybs']
                st['xr'] = load_x(xsB_v, "xsB", s, n, xr_rot)
                kch = [(ybs[k][0][:, 2:n], ybs[k][1], w3out_b[k]) for k in range(8)]
                banks, sl = out_proj_slices(kch, lambda k, m: WA3[:, W3OUT + k * D + m * 128: W3OUT + k * D + (m + 1) * 128], n, 2)
                st['banks'] = banks
                return sl

            def p3_bfinal_a(i, st):
                st['nsq'] = norm_residual_a(st['banks'], st['n'], 2)

            def p3_bfinal_b(i, st):
                s, n = st['s'], st['n']
                xt, xb = st['xr']
                sq, sqb = st['nsq']
                norm_residual_b(st['banks'], sq, sqb, xt, xb, n, 2, SC_GAIN + (1 * 2 + 1) * 8, s)
                store_x(xt, xb, xsA_v, "xsA", s, n, s + 2)

            def p3_A2(i, st, mid):
                mid()
                chunks3(st, 4, 8, lambda: None)

            run_pipeline3(len(tiles), prep3, prep3_b, lambda i, st, tick: chunks3(st, 0, 4, tick), p3_A2,
                          p3_bslices, p3_bfinal_a, p3_bfinal_b)
            ph3.close()
            set_fence()

        if nphase >= 4:
            ffn_phase(1, xsA_v, "xsA", outT_v, None, HALO - 2, HALO)

        S.emit(nc)
    return nc


def _chunk_cols(v):
    return np.ascontiguousarray(v.reshape(-1, 128).T)


def _rot_cols(w):
    r = w.copy()
    nh = w.shape[1] // 64
    for hd in range(nh):
        r[:, hd * 64:hd * 64 + 8] = w[:, hd * 64 + 8:hd * 64 + 16]
        r[:, hd * 64 + 8:hd * 64 + 16] = w[:, hd * 64:hd * 64 + 8]
    return r


_NC_CACHE = {}


def kernel(x, positions, mix_norm_pre, mix_norm_post, ffn_norm_pre, ffn_norm_post,
           ev_w_in, ev_a_conv_w, ev_a_conv_b, ev_a_ln_g, ev_a_ln_b, ev_sinks, ev_w_out,
           od_w_in, od_conv_w, od_w_out, ffn_w_up, ffn_conv_w, ffn_w_down):
    f32 = np.float32
    x = np.asarray(x, f32)
    positions = np.asarray(positions, np.int32)
    w_in = np.asarray(ev_w_in, f32)[0]
    q = w_in[:, 1024:1536]
    k = w_in[:, 1536:1664]
    v = w_in[:, 1664:1792]
    k0, k1 = k[:, 0:64], k[:, 64:128]
    k0r, k1r = _rot_cols(k0), _rot_cols(k1)
    w1in = np.ascontiguousarray(np.concatenate(
        [w_in[:, 0:1024], q, _rot_cols(q), k0, k0, k1, k1, k0r, k0r, k1r, k1r, v], axis=1))
    assert w1in.shape == (D, WIN_EXT)
    w1out = np.ascontiguousarray(np.asarray(ev_w_out, f32)[0])
    w3in = np.ascontiguousarray(np.asarray(od_w_in, f32)[0])
    w3out = np.ascontiguousarray(np.asarray(od_w_out, f32)[0])
    wup = np.ascontiguousarray(np.asarray(ffn_w_up, f32))
    wdn = np.ascontiguousarray(np.asarray(ffn_w_down, f32))
    small = np.zeros((128, SC_N), f32)
    gains = [mix_norm_pre, mix_norm_post, ffn_norm_pre, ffn_norm_post]
    for kind in range(4):
        for layer in range(2):
            c0 = SC_GAIN + (kind * 2 + layer) * 8
            small[:, c0:c0 + 8] = _chunk_cols(np.asarray(gains[kind], f32)[layer])
    c31 = np.asarray(ev_a_conv_w, f32)[0]
    for c in range(4):
        small[:, SC_C31 + c * 31: SC_C31 + (c + 1) * 31] = c31[:, c * 128:(c + 1) * 128].T
    small[:, SC_CB:SC_CB + 4] = _chunk_cols(np.asarray(ev_a_conv_b, f32)[0])
    small[:, SC_LNG:SC_LNG + 4] = _chunk_cols(np.asarray(ev_a_ln_g, f32)[0])
    small[:, SC_LNB:SC_LNB + 4] = _chunk_cols(np.asarray(ev_a_ln_b, f32)[0])
    sinks = np.asarray(ev_sinks, f32)[0]
    for cc in range(4):
        small[0:64, SC_SINK + cc] = sinks[2 * cc]
        small[64:128, SC_SINK + cc] = sinks[2 * cc + 1]
    odc = np.asarray(od_conv_w, f32)[0]
    for c in range(8):
        small[:, SC_ODC + c * 3: SC_ODC + c * 3 + 3] = odc[:, c * 128:(c + 1) * 128].T
    ffc = np.asarray(ffn_conv_w, f32)
    for layer in range(2):
        for c in range(NFF):
            c0 = SC_FFC + (layer * NFF + c) * 3
            small[:, c0:c0 + 3] = ffc[layer][:, c * 128:(c + 1) * 128].T
    inv_freq = (f32(500000.0) ** (-(np.arange(8, dtype=f32) * f32(2.0) / f32(16.0)))).astype(f32)
    for p in range(128):
        d = p % 64
        if d < 8:
            small[p, SC_FS] = -inv_freq[d]
        elif d < 16:
            small[p, SC_FS] = inv_freq[d - 8]
    small[:, SC_EPS_RMS] = 1e-6
    small[:, SC_EPS_LN] = 1e-5
    small[:, SC_HALFPI] = np.pi / 2
    kk = np.arange(128)[:, None]
    qq = np.arange(128)[None, :]
    m_d = np.where(kk <= qq, 0.0, NEG).astype(f32)
    m_p = np.where(kk > qq, 0.0, NEG).astype(f32)
    m_all = np.full((128, 128), NEG, f32)
    in_maps = []
    for core in range(NCORES):
        b, j = core // 4, core % 4
        start = j * TOK
        xT = np.zeros((D, T), f32)
        pos = np.zeros((T,), np.int32)
        lo = start - HALO
        if lo >= 0:
            xT[:] = x[b, lo:start + TOK, :].T
            pos[:] = positions[b, lo:start + TOK]
        else:
            xT[:, HALO:] = x[b, 0:TOK, :].T
            pos[HALO:] = positions[b, 0:TOK]
        sm = small.copy()
        sm[:, SC_HM] = 0.0 if j == 0 else 1.0
        m_f = m_all if j == 0 else m_p
        cst = np.concatenate([np.eye(128, dtype=f32), m_p, m_p, m_d, m_d, m_f, m_f, m_d, m_d], axis=1)
        in_maps.append({
            "xT": np.ascontiguousarray(xT),
            "posr": np.ascontiguousarray(np.broadcast_to(pos[None, :], (128, T))),
            "small": sm,
            "cst": np.ascontiguousarray(cst),
            "w1in": w1in, "w1out": w1out, "w3in": w3in, "w3out": w3out, "wup": wup, "wdn": wdn,
        })
    if _NC_CACHE.get("prep_only"):
        return in_maps
    if "nc" not in _NC_CACHE:
        _NC_CACHE["nc"] = build_program()
    res = run_bass_kernel_spmd(_NC_CACHE["nc"], in_maps, core_ids=list(range(NCORES)))
    out = np.empty((2, SEQ, D), f32)
    for core in range(NCORES):
        b, j = core // 4, core % 4
        out[b, j * TOK:(j + 1) * TOK, :] = res.results[core]["outT"].T
    return out
```

```python
import numpy as np
from contextlib import ExitStack
import concourse.bass as bass
import concourse.mybir as mybir
from concourse.bass_utils import run_bass_kernel_spmd

F32 = mybir.dt.float32
BF16 = mybir.dt.bfloat16
I32 = mybir.dt.int32
ALU = mybir.AluOpType
AF = mybir.ActivationFunctionType

NCORES = 8
D = 1024
SEQ = 16384
TOK = 4096
HALO = 256
T = TOK + HALO
DFF = 2816
NFF = 2 * DFF // 128
WIN_EXT = 2688
NEG = -30000.0
TWO_PI = 6.283185307179586
C1 = 6.28125
C2 = TWO_PI - C1

SC_GAIN = 0
SC_C31 = 64
SC_CB = 188
SC_LNG = 192
SC_LNB = 196
SC_SINK = 200
SC_ODC = 204
SC_FFC = 228
SC_FS = 492
SC_HM = 493
SC_EPS_RMS = 494
SC_EPS_LN = 495
SC_HALFPI = 496
SC_N = 500


class Buf:
    __slots__ = ("name", "w", "r", "serial")

    def __init__(self, name, serial=False):
        self.name = name
        self.w = None
        self.r = []
        self.serial = serial


class Op:
    __slots__ = ("eng", "fn", "deps", "ms", "is_dma", "dsem", "dcount", "used")


class Sched:
    ENGS = ("pe", "act", "dve", "pool", "sp")

    def __init__(self, n_dma_sems=12):
        self.q = {e: [] for e in self.ENGS}
        self.n_dma_sems = n_dma_sems
        self.dma_rr = {e: 0 for e in self.ENGS}
        self.dma_last = {}

    def add(self, eng, fn, reads=(), writes=(), dma=False, extra=()):
        op = Op()
        op.eng = eng
        op.fn = fn
        op.is_dma = dma
        op.ms = 0
        op.used = False
        op.dsem = None
        op.dcount = 0
        deps = list(extra)
        ser = [b for b in reads if b.serial]
        if ser:
            reads = [b for b in reads if not b.serial]
            writes = list(writes) + ser
        for b in reads:
            if b.w is not None:
                deps.append(b.w)
        for b in writes:
            if b.w is not None:
                deps.append(b.w)
            deps.extend(b.r)
        if dma:
            slot = self.dma_rr[eng]
            self.dma_rr[eng] = (slot + 1) % self.n_dma_sems
            prev = self.dma_last.get((eng, slot))
            if prev is not None:
                deps.append(prev)
                op.dcount = prev.dcount + 16
            else:
                op.dcount = 16
            op.dsem = (eng, slot)
            self.dma_last[(eng, slot)] = op
        for b in reads:
            b.r.append(op)
        for b in writes:
            b.w = op
            b.r = []
        op.deps = [d for d in set(deps)
                   if d is not op and not (eng == "pe" and d.eng == "pe" and not d.is_dma and not dma)]
        for d in op.deps:
            d.used = True
        self.q[eng].append(op)
        return op

    def emit(self, nc):
        for e in self.ENGS:
            n = 0
            for op in self.q[e]:
                if op.used and not op.is_dma:
                    n += 1
                    op.ms = n
            assert n < 60000, (e, n)
        with ExitStack() as es:
            csem = {e: es.enter_context(nc.semaphore("c_" + e)) for e in ("pe", "act", "dve", "pool")}
            dsem = {}
            for key in self.dma_last:
                dsem[key] = es.enter_context(nc.semaphore("d_%s%d" % key))
            block = es.enter_context(nc.Block())
            q = self.q
            dma_last = self.dma_last

            def run(e, eng):
                waited = {}
                for op in q[e]:
                    need = {}
                    for d in op.deps:
                        if d.is_dma:
                            k, v = ("d",) + d.dsem, d.dcount
                        else:
                            k, v = ("c", d.eng), d.ms
                        if need.get(k, 0) < v:
                            need[k] = v
                    for k, v in need.items():
                        if waited.get(k, 0) >= v:
                            continue
                        waited[k] = v
                        s = csem[k[1]] if k[0] == "c" else dsem[(k[1], k[2])]
                        eng.wait_ge(s, v)
                    ins = op.fn(eng)
                    if op.is_dma:
                        ins.then_inc(dsem[op.dsem], 16)
                    elif op.used:
                        ins.then_inc(csem[e], 1)
                if e == "sp":
                    for key, op in dma_last.items():
                        if waited.get(("d",) + key, 0) < op.dcount:
                            eng.wait_ge(dsem[key], op.dcount)

            @block.tensor
            def _(eng):
                run("pe", eng)

            @block.scalar
            def _(eng):
                run("act", eng)

            @block.vector
            def _(eng):
                run("dve", eng)

            @block.gpsimd
            def _(eng):
                run("pool", eng)

            @block.sync
            def _(eng):
                run("sp", eng)


class Rot:
    mk = Buf

    def __init__(self, alloc, name, k, shape, dt):
        self.items = [(alloc(name + str(i), shape, dt), Rot.mk(name + str(i))) for i in range(k)]
        self.i = 0

    def next(self):
        it = self.items[self.i]
        self.i = (self.i + 1) % len(self.items)
        return it


class _Stop(Exception):
    pass


def build_program(debug=False, nphase=4, maxtiles=None, stage=99):
    nc = bass.Bass("TRN2", target_bir_lowering=False)
    S = Sched()

    def din(name, shape, dt=F32):
        return nc.dram_tensor(name, list(shape), dt, kind="ExternalInput").ap()

    xT = din("xT", [D, T])
    posr = din("posr", [128, T], I32)
    small_d = din("small", [128, SC_N])
    cst_d = din("cst", [128, 128 + 2 * 512])
    w1in_d = din("w1in", [D, WIN_EXT])
    w1out_d = din("w1out", [D, D])
    w3in_d = din("w3in", [D, 3 * D])
    w3out_d = din("w3out", [D, D])
    wup_d = din("wup", [2, D, 2 * DFF])
    wdn_d = din("wdn", [2, DFF, D])
    outT = nc.dram_tensor("outT", [D, TOK], F32, kind="ExternalOutput").ap()
    kind = {"kind": "ExternalOutput"} if debug else {}
    xsA = nc.dram_tensor("xsA", [D, T], F32, **kind).ap()
    xsB = nc.dram_tensor("xsB", [D, T], F32, **kind).ap()

    def tview(ap):
        return ap.rearrange("(c p) t -> p c t", p=128)

    xT_v, xsA_v, xsB_v, outT_v = tview(xT), tview(xsA), tview(xsB), tview(outT)
    dbufs = {"xsA": [Buf("xsA%d" % i) for i in range(T // 256 + 1)],
             "xsB": [Buf("xsB%d" % i) for i in range(T // 256 + 1)]}

    def dblocks(key, s, n):
        return [dbufs[key][i] for i in range(s // 256, (s + n - 1) // 256 + 1)]

    fence = []

    def mkbuf(name):
        b = Buf(name)
        b.r = list(fence)
        return b

    def set_fence():
        fence[:] = []
        for e in ("pe", "act", "dve", "pool"):
            for op in reversed(S.q[e]):
                if not op.is_dma:
                    fence.append(op)
                    break

    def chk(k):
        if stage < k:
            raise _Stop()

    with ExitStack() as es:
        scope = [es]

        def sb(name, shape, dt):
            return scope[0].enter_context(nc.sbuf_tensor(name, list(shape), dt))

        Rot.mk = staticmethod(mkbuf)
        small = sb("smallt", [128, SC_N], F32)
        cst = sb("cstt", [128, 128 + 2 * 512], BF16)
        ones = sb("ones", [128, 128], BF16)
        olo = sb("olo", [128, 128], BF16)
        ohi = sb("ohi", [128, 128], BF16)
        esink = sb("esink", [128, 4], F32)
        b_small, b_cst, b_ones, b_olo, b_ohi, b_esink = (Buf(n) for n in ("small", "cst", "ones", "olo", "ohi", "esink"))

        psum = [es.enter_context(nc.psum_tensor("ps%d" % i, [128, 512], F32)) for i in range(8)]
        pbufs = [Buf("bank%d" % i, serial=True) for i in range(8)]
        pbusy = [False] * 8
        pfreeq = list(range(8))

        def palloc():
            if not pfreeq:
                raise RuntimeError("out of PSUM banks")
            i = pfreeq.pop(0)
            pbusy[i] = True
            return i

        def pfree(i):
            assert pbusy[i]
            pbusy[i] = False
            pfreeq.append(i)

        def sc(col, n=1):
            return small[:, col:col + n]

        def pv3(bk):
            return psum[bk][:, :].rearrange("p (a b) -> p a b", a=2)

        ident = cst[:, 0:128]
        mask_a = cst[:, 128:640]
        mask_af = cst[:, 640:1152]

        def DMA(q, out, in_, reads=(), writes=()):
            return S.add(q, lambda e, o=out, i=in_: e.dma_start(out=o, in_=i), reads, writes, dma=True)

        def MM(out, lhsT, rhs, start, stop, reads, writes, sgc=False):
            if sgc:
                return S.add("pe", lambda e, o=out, l=lhsT, r=rhs, a=start, b=stop:
                             e.matmul(o, lhsT=l, rhs=r, start=a, stop=b, skip_group_check=True), reads, writes)
            return S.add("pe", lambda e, o=out, l=lhsT, r=rhs, a=start, b=stop: e.matmul(o, lhsT=l, rhs=r, start=a, stop=b),
                         reads, writes)

        def ACT(out, in_, func, reads, writes, bias=None, scale=None):
            def fn(e, o=out, i=in_, f=func, b=bias, s=scale):
                kw = {}
                if b is not None:
                    kw["bias"] = b
                if s is not None:
                    kw["scale"] = s
                return e.activation(out=o, in_=i, func=f, **kw)
            return S.add("act", fn, reads, writes)

        def TT(eng, out, in0, in1, op, reads, writes):
            return S.add(eng, lambda e, o=out, a=in0, b=in1, p=op: e.tensor_tensor(out=o, in0=a, in1=b, op=p), reads, writes)

        def TS(eng, out, in0, s1, op0, reads, writes, s2=None, op1=None):
            def fn(e, o=out, a=in0, x1=s1, x2=s2, p0=op0, p1=op1):
                if p1 is None:
                    return e.tensor_scalar(out=o, in0=a, scalar1=x1, scalar2=None, op0=p0)
                return e.tensor_scalar(out=o, in0=a, scalar1=x1, scalar2=x2, op0=p0, op1=p1)
            return S.add(eng, fn, reads, writes)

        def STT(out, in0, scalar, in1, op0, op1, reads, writes):
            return S.add("dve", lambda e, o=out, a=in0, s=scalar, b=in1, p0=op0, p1=op1:
                         e.scalar_tensor_tensor(out=o, in0=a, scalar=s, in1=b, op0=p0, op1=p1), reads, writes)

        def COPY(eng, out, in_, reads, writes):
            if eng == "act":
                return S.add(eng, lambda e, o=out, i=in_: e.activation(out=o, in_=i, func=AF.Copy), reads, writes)
            return S.add(eng, lambda e, o=out, i=in_: e.tensor_copy(out=o, in_=i), reads, writes)

        def RECIP(out, in_, reads, writes):
            return S.add("dve", lambda e, o=out, i=in_: e.reciprocal(out=o, in_=i), reads, writes)

        def MEMSET(eng, ap, val, writes):
            return S.add(eng, lambda e, a=ap, v=val: e.memset(a, v), (), writes)

        DMA("sp", small[:], small_d, writes=[b_small])
        DMA("pool", cst[:], cst_d, writes=[b_cst])
        MEMSET("pool", ones[:], 1.0, [b_ones])
        MEMSET("pool", olo[:], 0.0, [b_olo])
        MEMSET("pool", olo[:, 0:64], 1.0, [b_olo])
        MEMSET("pool", ohi[:], 0.0, [b_ohi])
        MEMSET("pool", ohi[:, 64:128], 1.0, [b_ohi])
        ACT(esink[:], sc(SC_SINK, 4), AF.Exp, [b_small], [b_esink])

        NT = 256
        xp_rot = Rot(sb, "xp", 1, [128, 8, NT], F32)
        xr_rot = Rot(sb, "xr", 1, [128, 8, NT], F32)
        sq_rot = Rot(sb, "sq", 2, [128, 8, NT], BF16)
        h_rot = Rot(sb, "h", 2, [128, 8, NT], BF16)
        stat_rot = Rot(sb, "stat", 3, [128, NT], F32)
        tmp_rot = Rot(sb, "tmp", 2, [128, NT], F32)

        def load_weight_rows(WA, dram2d, col0, nchunks, ncols, tag):
            bufs = []
            for c in range(nchunks):
                b = mkbuf("%s%d" % (tag, c))
                DMA("pool", WA[:, col0 + c * ncols: col0 + (c + 1) * ncols], dram2d[c * 128:(c + 1) * 128, :],
                    writes=[b])
                bufs.append(b)
            return bufs

        def rms_rstd(sq_chunks, n, reads):
            bk = palloc()
            nchk = len(sq_chunks)
            for c, ap in enumerate(sq_chunks):
                MM(psum[bk][:, 0:n], ones[:], ap, c == 0, c == nchk - 1, list(reads) + [b_ones], [pbufs[bk]])
            sd, sdb = stat_rot.next()
            ACT(sd[:, 0:n], psum[bk][:, 0:n], AF.Sqrt, [pbufs[bk], b_small], [sdb], bias=sc(SC_EPS_RMS), scale=1.0 / D)
            pfree(bk)
            rs, rsb = stat_rot.next()
            RECIP(rs[:, 0:n], sd[:, 0:n], [sdb], [rsb])
            return rs, rsb

        def load_x(src_v, src_key, s, n, pool):
            xt, xb = pool.next()
            rd = dblocks(src_key, s, n) if src_key else []
            DMA("sp", xt[:, :, 0:n], src_v[:, :, s:s + n], reads=rd, writes=[xb])
            return xt, xb

        def prenorm_a(xt, xb, n):
            sq, sqb = sq_rot.next()
            ACT(sq[:, :, 0:n], xt[:, :, 0:n], AF.Square, [xb], [sqb])
            return sq, sqb

        def prenorm_b(xt, xb, sq, sqb, n, gcol):
            rs, rsb = rms_rstd([sq[:, c, 0:n] for c in range(8)], n, [sqb])
            h, hb = h_rot.next()
            for c in range(8):
                STT(h[:, c, 0:n], xt[:, c, 0:n], sc(gcol + c), rs[:, 0:n], ALU.mult, ALU.mult, [xb, rsb, b_small], [hb])
            return h, hb

        def out_proj_slices(kchunks, wfn, n, c0, order=None):
            w = n - c0
            nk = len(kchunks)
            banks = []

            order = list(range(nk)) if order is None else order

            def mk(pos, k):
                def fn():
                    if pos == 0:
                        for mp in range(4):
                            banks.append(palloc())
                    ap, kb, wb = kchunks[k]
                    for mp in range(4):
                        bk = banks[mp]
                        for hf in range(2):
                            MM(psum[bk][:, hf * 256:hf * 256 + w], wfn(k, 2 * mp + hf), ap, pos == 0 and hf == 0, pos == nk - 1,
                               [kb, wb], [pbufs[bk]], sgc=True)
                return fn
            return banks, [mk(pos, k) for pos, k in enumerate(order)]

        def norm_residual(banks, xt, xb, n, c0, gcol, s):
            w = n - c0
            sq, sqb = norm_residual_a(banks, n, c0)
            norm_residual_b(banks, sq, sqb, xt, xb, n, c0, gcol, s)

        def norm_residual_a(banks, n, c0):
            w = n - c0
            sq, sqb = sq_rot.next()
            for mp, bk in enumerate(banks):
                ACT(sq[:, 2 * mp:2 * mp + 2, 0:w], pv3(bk)[:, :, 0:w], AF.Square, [pbufs[bk]], [sqb])
            return sq, sqb

        def norm_residual_b(banks, sq, sqb, xt, xb, n, c0, gcol, s):
            w = n - c0
            rs, rsb = rms_rstd([sq[:, c, 0:w] for c in range(8)], w, [sqb])
            for mp, bk in enumerate(banks):
                for hf in range(2):
                    c = 2 * mp + hf
                    tm, tmb = tmp_rot.next()
                    STT(tm[:, 0:w], psum[bk][:, hf * 256:hf * 256 + w], sc(gcol + c), rs[:, 0:w], ALU.mult, ALU.mult,
                        [pbufs[bk], rsb, b_small], [tmb])
                    TT("dve", xt[:, c, c0:n], tm[:, 0:w], xt[:, c, c0:n], ALU.add, [tmb, xb], [xb])
                pfree(bk)
            if s < HALO:
                hc = min(HALO - s, n)
                TS("dve", xt[:, :, 0:hc], xt[:, :, 0:hc], sc(SC_HM), ALU.mult, [xb, b_small], [xb])

        def out_proj_norm_residual(kchunks, wfn, xt, xb, n, c0, gcol, s):
            banks, sl = out_proj_slices(kchunks, wfn, n, c0)
            for f in sl:
                f()
            norm_residual(banks, xt, xb, n, c0, gcol, s)

        def store_x(xt, xb, dst_v, dst_key, s, n, wlo, dst_off=0):
            wr = dblocks(dst_key, wlo, s + n - wlo) if dst_key else []
            DMA("sp", dst_v[:, :, wlo - dst_off:s + n - dst_off], xt[:, :, wlo - s:n], reads=[xb], writes=wr)

        def run_pipeline(nt, prep, A1, A2, Bst):
            if nt == 0:
                return
            st = {0: prep(0)}
            A1(0, st[0])
            if nt > 1:
                st[1] = prep(1)
            A2(0, st[0])
            for i in range(nt):
                if i + 1 < nt:
                    A1(i + 1, st[i + 1])
                if i + 2 < nt:
                    st[i + 2] = prep(i + 2)
                Bst(i, st[i])
                del st[i]
                if i + 1 < nt:
                    A2(i + 1, st[i + 1])

        def run_pipeline2(nt, prep_a, prep_b, nunits, unit, unit_end, bslices, bfinal_a, bfinal_b, sched, pa_at, pb_at, fa_at):
            if nt == 0:
                return
            st = {0: prep_b(prep_a(0))}
            for j in range(nunits):
                unit(0, st[0], j)
                if j == pa_at and nt > 1:
                    st[1] = prep_a(1)
                if j == pb_at and nt > 1:
                    st[1] = prep_b(st[1])
            unit_end(0, st[0])
            for i in range(nt):
                pending = bslices(i, st[i])
                idx = 0
                if i + 1 < nt:
                    for j in range(nunits):
                        unit(i + 1, st[i + 1], j)
                        for _ in range(sched.get(j, 0)):
                            if idx < len(pending):
                                pending[idx]()
                                idx += 1
                        if j == fa_at:
                            while idx < len(pending):
                                pending[idx]()
                                idx += 1
                            bfinal_a(i, st[i])
                        if j == pa_at and i + 2 < nt:
                            st[i + 2] = prep_a(i + 2)
                        if j == pb_at and i + 2 < nt:
                            st[i + 2] = prep_b(st[i + 2])
                    unit_end(i + 1, st[i + 1])
                else:
                    while idx < len(pending):
                        pending[idx]()
                        idx += 1
                    bfinal_a(i, st[i])
                bfinal_b(i, st[i])
                del st[i]

        def run_pipeline3(nt, prep_a, prep_b, A1, A2, bslices, bfinal_a, bfinal_b, prep_tick=0):
            if nt == 0:
                return

            def noop():
                pass

            st = {0: prep_b(prep_a(0))}
            A1(0, st[0], noop)
            if nt > 1:
                st[1] = prep_b(prep_a(1))
            A2(0, st[0], noop)
            for i in range(nt):
                pending = bslices(i, st[i])
                state = {'idx': 0, 'calls': 0, 'prepped': False}

                def do_prep(i=i, state=state):
                    if not state['prepped'] and i + 2 < nt:
                        st[i + 2] = prep_a(i + 2)
                    state['prepped'] = True

                def tick(pending=pending, state=state, do_prep=do_prep):
                    if state['calls'] == prep_tick:
                        do_prep()
                    state['calls'] += 1
                    if state['idx'] < len(pending):
                        pending[state['idx']]()
                        state['idx'] += 1

                if prep_tick == 0:
                    do_prep()
                if i + 1 < nt:
                    A1(i + 1, st[i + 1], tick)
                do_prep()
                while state['idx'] < len(pending):
                    pending[state['idx']]()
                    state['idx'] += 1
                bfinal_a(i, st[i])

                def mid(i=i):
                    if i + 2 < nt:
                        st[i + 2] = prep_b(st[i + 2])
                    bfinal_b(i, st[i])

                if i + 1 < nt:
                    A2(i + 1, st[i + 1], mid)
                else:
                    mid()
                del st[i]

        ph1 = es.enter_context(ExitStack())
        scope[0] = ph1
        W1IN, W1OUT, DIAG = 0, 8 * WIN_EXT, 8 * WIN_EXT + 8 * D
        WA1 = sb("WA1", [128, DIAG + 124 * 128], BF16)
        WA1v = WA1[:, W1IN:W1IN + 8 * WIN_EXT].rearrange("p (c m) -> p c m", c=8)
        w1in_v = w1in_d.rearrange("(c p) m -> p c m", p=128)
        win_piece = {}
        for nm, (c0, c1) in (("v", (2560, 2688)), ("k", (2048, 2560)), ("q", (1024, 2048)), ("glu", (0, 1024))):
            b = mkbuf("w1in_" + nm)
            for cc0 in range(c0, c1, 512):
                cc1 = min(cc0 + 512, c1)
                DMA("pool", WA1v[:, :, cc0:cc1], w1in_v[:, :, cc0:cc1], writes=[b])
            win_piece[nm] = b

        def winb(off):
            return win_piece["v" if off >= 2560 else "k" if off >= 2048 else "q" if off >= 1024 else "glu"]
        wout_b = load_weight_rows(WA1, w1out_d, W1OUT, 8, D, "w1out")
        b_diag = mkbuf("diag")
        identh = sb("identh", [128, 128], BF16)
        b_identh = mkbuf("identh")
        TS("dve", identh[:], ident, 0.5, ALU.mult, [b_cst], [b_identh])
        diag_state = {'done': False}

        def build_diags():
            if diag_state['done']:
                return
            diag_state['done'] = True
            for c in range(4):
                for j in range(31):
                    col = DIAG + (c * 31 + j) * 128
                    ACT(WA1[:, col:col + 128], identh[:], AF.Identity, [b_identh, b_small], [b_diag],
                        scale=sc(SC_C31 + c * 31 + j))

        def w1in(c, off, m):
            return WA1[:, W1IN + c * WIN_EXT + off: W1IN + c * WIN_EXT + off + m]

        NB = NT // 128
        abuf = [sb("abuf%d" % c, [128, 30 + NT], BF16) for c in range(4)]
        b_abuf = [mkbuf("abuf%d" % c) for c in range(4)]
        krot = [sb("krot%d" % g, [128, 128 + NT], BF16) for g in range(2)]
        b_krot = [mkbuf("krot%d" % g) for g in range(2)]
        vlo = [sb("vlo%d" % g, [128, (NB + 1) * 128], BF16) for g in range(2)]
        vhi = [sb("vhi%d" % g, [128, (NB + 1) * 128], BF16) for g in range(2)]
        b_v = [mkbuf("v%d" % g) for g in range(2)]
        for c in range(4):
            MEMSET("pool", abuf[c][:], 0.0, [b_abuf[c]])
        for g in range(2):
            MEMSET("pool", krot[g][:], 0.0, [b_krot[g]])
            MEMSET("pool", vlo[g][:], 0.0, [b_v[g]])
            MEMSET("pool", vhi[g][:], 0.0, [b_v[g]])
        qrot_rot = Rot(sb, "qrot", 8, [128, NT], BF16)
        pT_rot = Rot(sb, "pT", 8, [128, 512], BF16)
        pos_rot = Rot(sb, "posi", 2, [128, NT], I32)
        ki_rot = Rot(sb, "ki", 2, [128, NT], I32)
        cs_rot = Rot(sb, "cs", 4, [128, NT], F32)
        rt_rot = Rot(sb, "rt", 7, [128, NT], F32)
        tg_rot = Rot(sb, "tg", 2, [128, NT], F32)
        ac_rot = Rot(sb, "ac", 4, [128, NT], F32)
        acb_rot = Rot(sb, "acb", 4, [128, NT], BF16)
        sq2_rot = Rot(sb, "sq2", 4, [128, NT], BF16)
        mix_rot = Rot(sb, "mix", 8, [128, NT], BF16)

        ntiles1 = T // NT if maxtiles is None else (maxtiles + (nphase - 1))
        def prep1(ti):
            s, n = ti * NT, NT
            xt, xb = load_x(xT_v, None, s, n, xp_rot)
            pi_, pib = pos_rot.next()
            DMA("sp", pi_[:, 0:n], posr[:, s:s + n], writes=[pib])
            posf, posfb = rt_rot.next()
            COPY("dve", posf[:, 0:n], pi_[:, 0:n], [pib], [posfb])
            ang, angb = rt_rot.next()
            TS("dve", ang[:, 0:n], posf[:, 0:n], sc(SC_FS), ALU.mult, [posfb, b_small], [angb])
            kq, kqb = rt_rot.next()
            TS("dve", kq[:, 0:n], ang[:, 0:n], 1.0 / TWO_PI, ALU.mult, [angb], [kqb])
            ki, kib = ki_rot.next()
            COPY("dve", ki[:, 0:n], kq[:, 0:n], [kqb], [kib])
            kf, kfb = rt_rot.next()
            COPY("dve", kf[:, 0:n], ki[:, 0:n], [kib], [kfb])
            r1, r1b = rt_rot.next()
            STT(r1[:, 0:n], kf[:, 0:n], -C1, ang[:, 0:n], ALU.mult, ALU.add, [kfb, angb], [r1b])
            r2, r2b = rt_rot.next()
            STT(r2[:, 0:n], kf[:, 0:n], -C2, r1[:, 0:n], ALU.mult, ALU.add, [kfb, r1b], [r2b])
            r3, r3b = rt_rot.next()
            TS("dve", r3[:, 0:n], r2[:, 0:n], -3.1415925, ALU.max, [r2b], [r3b], s2=3.1415925, op1=ALU.min)
            Sf, Sfb = cs_rot.next()
            ACT(Sf[:, 0:n], r3[:, 0:n], AF.Sin, [r3b], [Sfb])
            ar, arb = rt_rot.next()
            ACT(ar[:, 0:n], r3[:, 0:n], AF.Abs, [r3b], [arb])
            Cf, Cfb = cs_rot.next()
            ACT(Cf[:, 0:n], ar[:, 0:n], AF.Sin, [arb, b_small], [Cfb], bias=sc(SC_HALFPI), scale=-1.0)

            sq, sqb = prenorm_a(xt, xb, n)
            return dict(s=s, n=n, xt=xt, xb=xb, sq=sq, sqb=sqb, Cf=Cf, Cfb=Cfb, Sf=Sf, Sfb=Sfb)

        def prep1_b(st):
            st['h'], st['hb'] = prenorm_b(st['xt'], st['xb'], st['sq'], st['sqb'], st['n'], SC_GAIN + (0 * 2 + 0) * 8)
            build_diags()
            return st

        def p1_A1(ti, st, tick):
            s, n, h, hb = st['s'], st['n'], st['h'], st['hb']
            Cf, Cfb, Sf, Sfb = st['Cf'], st['Cfb'], st['Sf'], st['Sfb']
            def proj_into(bk, half, off):
                for c in range(8):
                    MM(psum[bk][:, half * 256:half * 256 + n], w1in(c, off, 128), h[:, c, 0:n], c == 0, c == 7,
                       [hb, winb(off)], [pbufs[bk]])

            bk = palloc()
            for b in range(NB):
                for c in range(8):
                    MM(psum[bk][:, b * 128:(b + 1) * 128], h[:, c, b * 128:(b + 1) * 128], w1in(c, 2560, 128),
                       c == 0, c == 7, [hb, winb(2560)], [pbufs[bk]])
            for b in range(NB):
                for g in range(2):
                    COPY("act", vlo[g][:, (b + 1) * 128:(b + 1) * 128 + 64],
                         psum[bk][:, b * 128 + g * 64:b * 128 + g * 64 + 64], [pbufs[bk]], [b_v[g]])
                    COPY("act", vhi[g][:, (b + 1) * 128 + 64:(b + 2) * 128],
                         psum[bk][:, b * 128 + g * 64:b * 128 + g * 64 + 64], [pbufs[bk]], [b_v[g]])
            pfree(bk)
            tick()

            def rope(off_x, off_r, out_ap, out_buf):
                bk = palloc()
                proj_into(bk, 0, off_x)
                proj_into(bk, 1, off_r)
                t1, t1b = tmp_rot.next()
                TT("dve", t1[:, 0:n], psum[bk][:, 0:n], Cf[:, 0:n], ALU.mult, [pbufs[bk], Cfb], [t1b])
                t2, t2b = tmp_rot.next()
                TT("dve", t2[:, 0:n], psum[bk][:, 256:256 + n], Sf[:, 0:n], ALU.mult, [pbufs[bk], Sfb], [t2b])
                pfree(bk)
                TT("dve", out_ap, t1[:, 0:n], t2[:, 0:n], ALU.add, [t1b, t2b], [out_buf])

            for g in range(2):
                rope(2048 + g * 128, 2304 + g * 128, krot[g][:, 128:128 + n], b_krot[g])
                tick()
            qr_t = []
            for cq in range(4):
                qt, qb = qrot_rot.next()
                rope(1024 + cq * 128, 1536 + cq * 128, qt[:, 0:n], qb)
                qr_t.append((qt, qb))
                tick()
            for c in range(4):
                bk = palloc()
                proj_into(bk, 0, 512 + c * 128)
                proj_into(bk, 1, c * 128)
                tg, tgb = tg_rot.next()
                ACT(tg[:, 0:n], psum[bk][:, 0:n], AF.Tanh, [pbufs[bk]], [tgb], scale=0.5)
                STT(abuf[c][:, 30:30 + n], tg[:, 0:n], 1.0, psum[bk][:, 256:256 + n], ALU.add, ALU.mult,
                    [tgb, pbufs[bk]], [b_abuf[c]])
                pfree(bk)
                tick()
            pTs = {}
            for g in range(2):
                for b in range(NB):
                    gblk = ti * NB + b
                    for half in range(2):
                        bk = palloc()
                        msk = mask_af if gblk == HALO // 128 else mask_a
                        MM(psum[bk][:, :], ident, msk, True, False, [b_cst], [pbufs[bk]])
                        for kb in range(2):
                            kcol = (b + kb) * 128
                            for ii in range(2):
                                qt, qb = qr_t[2 * g + ii]
                                col = (kb * 2 + ii) * 128
                                MM(psum[bk][:, col:col + 128], krot[g][half * 64:(half + 1) * 64, kcol:kcol + 128],
                                   qt[half * 64:(half + 1) * 64, b * 128:(b + 1) * 128], False, kb == 1 and ii == 1,
                                   [b_krot[g], qb], [pbufs[bk]])
                        p_t, p_b = pT_rot.next()
                        ACT(p_t[:, :], psum[bk][:, :], AF.Exp, [pbufs[bk]], [p_b], scale=0.125)
                        pfree(bk)
                        pTs[(g, b, half)] = (p_t, p_b)
                        tick()
            st['pTs'] = pTs
            st['proj_into'] = proj_into

        def p1_A2(ti, st, tick):
            s, n, h, hb = st['s'], st['n'], st['h'], st['hb']
            Cf, Cfb, Sf, Sfb = st['Cf'], st['Cfb'], st['Sf'], st['Sfb']
            pTs, proj_into = st['pTs'], st['proj_into']
            tick()
            acs = []
            for cp in range(2):
                bk = palloc()
                for hf in range(2):
                    c = 2 * cp + hf
                    for j in range(31):
                        col = DIAG + (c * 31 + j) * 128
                        MM(psum[bk][:, hf * 256:hf * 256 + n], WA1[:, col:col + 128], abuf[c][:, j:j + n], j == 0, j == 30,
                           [b_diag, b_abuf[c]], [pbufs[bk]])
                for hf in range(2):
                    c = 2 * cp + hf
                    src = psum[bk][:, hf * 256:hf * 256 + n]
                    ac, acb_ = ac_rot.next()
                    ACT(ac[:, 0:n], src, AF.Identity, [pbufs[bk], b_small], [acb_], bias=sc(SC_CB + c))
                    s2, s2b = sq2_rot.next()
                    ACT(s2[:, 0:n], src, AF.Square, [pbufs[bk], b_small], [s2b], bias=sc(SC_CB + c))
                    a16, a16b = acb_rot.next()
                    COPY("dve", a16[:, 0:n], ac[:, 0:n], [acb_], [a16b])
                    acs.append((ac, acb_, a16, a16b, s2, s2b))
                    COPY("dve", abuf[c][:, 0:30], abuf[c][:, n:n + 30], [b_abuf[c]], [b_abuf[c]])
                pfree(bk)
            ochunks = []
            for cc in range(4):
                g = cc // 2
                iA = 2 * (cc % 2)
                bk = palloc()
                ii = cc % 2
                for b in range(NB):
                    k = 0
                    for kb in range(2):
                        vcol = (b + kb) * 128
                        col = (kb * 2 + ii) * 128
                        for hh, vt in enumerate((vlo[g], vhi[g])):
                            p_t, p_b = pTs[(g, b, hh)]
                            MM(psum[bk][:, b * 128:(b + 1) * 128], vt[:, vcol:vcol + 128],
                               p_t[:, col:col + 128], k == 0, k == 3, [b_v[g], p_b], [pbufs[bk]])
                            k += 1
                for b in range(NB):
                    k = 0
                    for kb in range(2):
                        col = (kb * 2 + ii) * 128
                        for hh, ot in enumerate((olo, ohi)):
                            p_t, p_b = pTs[(g, b, hh)]
                            MM(psum[bk][:, 256 + b * 128:256 + (b + 1) * 128], ot[:],
                               p_t[:, col:col + 128], k == 0, k == 3, [b_olo, b_ohi, p_b], [pbufs[bk]])
                            k += 1
                dn, dnb = stat_rot.next()
                ACT(dn[:, 0:n], psum[bk][:, 256:256 + n], AF.Identity, [pbufs[bk], b_esink], [dnb], bias=esink[:, cc:cc + 1])
                rd_, rdb = stat_rot.next()
                RECIP(rd_[:, 0:n], dn[:, 0:n], [dnb], [rdb])
                mo, mob = mix_rot.next()
                TT("dve", mo[:, 0:n], psum[bk][:, 0:n], rd_[:, 0:n], ALU.mult, [pbufs[bk], rdb], [mob])
                pfree(bk)
                ochunks.append((mo, mob))
            for g in range(2):
                COPY("dve", krot[g][:, 0:128], krot[g][:, n:n + 128], [b_krot[g]], [b_krot[g]])
                COPY("dve", vlo[g][:, 0:128], vlo[g][:, NB * 128:(NB + 1) * 128], [b_v[g]], [b_v[g]])
                COPY("dve", vhi[g][:, 0:128], vhi[g][:, NB * 128:(NB + 1) * 128], [b_v[g]], [b_v[g]])
            bk = palloc()
            for c in range(4):
                MM(psum[bk][:, 0:n], ones[:], acs[c][2][:, 0:n], c == 0, c == 3, [acs[c][3], b_ones], [pbufs[bk]])
            for c in range(4):
                MM(psum[bk][:, 256:256 + n], ones[:], acs[c][4][:, 0:n], c == 0, c == 3, [acs[c][5], b_ones], [pbufs[bk]])
            mean, meanb = stat_rot.next()
            ACT(mean[:, 0:n], psum[bk][:, 0:n], AF.Copy, [pbufs[bk]], [meanb], scale=1.0 / 512)
            msq, msqb = tmp_rot.next()
            ACT(msq[:, 0:n], psum[bk][:, 0:n], AF.Square, [pbufs[bk]], [msqb], scale=1.0 / 512)
            var, varb = tmp_rot.next()
            STT(var[:, 0:n], psum[bk][:, 256:256 + n], 1.0 / 512, msq[:, 0:n], ALU.mult, ALU.subtract, [pbufs[bk], msqb], [varb])
            pfree(bk)
            sd, sdb = tmp_rot.next()
            ACT(sd[:, 0:n], var[:, 0:n], AF.Sqrt, [varb, b_small], [sdb], bias=sc(SC_EPS_LN))
            rl, rlb = stat_rot.next()
            RECIP(rl[:, 0:n], sd[:, 0:n], [sdb], [rlb])
            achunks = []
            for c in range(4):
                ac, acb_ = acs[c][0], acs[c][1]
                t1, t1b = tmp_rot.next()
                TT("dve", t1[:, 0:n], ac[:, 0:n], mean[:, 0:n], ALU.subtract, [acb_, meanb], [t1b])
                t2, t2b = tmp_rot.next()
                STT(t2[:, 0:n], t1[:, 0:n], sc(SC_LNG + c), rl[:, 0:n], ALU.mult, ALU.mult, [t1b, rlb, b_small], [t2b])
                ma, mab = mix_rot.next()
                ACT(ma[:, 0:n], t2[:, 0:n], AF.Silu, [t2b, b_small], [mab], bias=sc(SC_LNB + c))
                achunks.append((ma, mab))
            st['mixk'] = achunks + ochunks

        def p1_bslices(ti, st):
            s, n = st['s'], st['n']
            st['xr'] = load_x(xT_v, None, s, n, xr_rot)
            mixk = st['mixk']
            kch = [(mixk[k][0][:, 0:n], mixk[k][1], wout_b[k]) for k in range(8)]
            banks, sl = out_proj_slices(kch, lambda k, m: WA1[:, W1OUT + k * D + m * 128: W1OUT + k * D + (m + 1) * 128], n, 0,
                                        order=[4, 5, 6, 7, 0, 1, 2, 3])
            st['banks'] = banks
            return sl

        def p1_bfinal_a(ti, st):
            st['nsq'] = norm_residual_a(st['banks'], st['n'], 0)

        def p1_bfinal_b(ti, st):
            s, n = st['s'], st['n']
            xt, xb = st['xr']
            sq, sqb = st['nsq']
            norm_residual_b(st['banks'], sq, sqb, xt, xb, n, 0, SC_GAIN + (1 * 2 + 0) * 8, s)
            store_x(xt, xb, xsA_v, "xsA", s, n, s)

        run_pipeline3(ntiles1, prep1, prep1_b, p1_A1, p1_A2, p1_bslices, p1_bfinal_a, p1_bfinal_b, prep_tick=7)
        ph1.close()
        set_fence()

        def ffn_phase(layer, src_v, src_key, dst_v, dst_key, s0, dst_off):
            ph = es.enter_context(ExitStack())
            scope[0] = ph
            WUP, WDN = 0, 8 * 2 * DFF
            WA = sb("WAF%d" % layer, [128, WDN + 22 * D], BF16)
            WAv = WA[:, WUP:WUP + 8 * 2 * DFF].rearrange("p (c m) -> p c m", c=8)
            wup_v = wup_d[layer].rearrange("(c p) m -> p c m", p=128)
            up_b = []
            for pc in range(11):
                b = mkbuf("wup%d_p%d" % (layer, pc))
                for base in (0, DFF):
                    c0 = base + pc * 256
                    DMA("pool", WAv[:, :, c0:c0 + 256], wup_v[:, :, c0:c0 + 256], writes=[b])
                up_b.append(b)
            dn_b = load_weight_rows(WA, wdn_d[layer], WDN, 22, D, "wdn%d_" % layer)
            ub_rot = Rot(sb, "ub%d_" % layer, 2, [128, 2, NT], F32)
            ya_rot = Rot(sb, "ya%d_" % layer, 3, [128, NT], F32)
            yy_rot = Rot(sb, "yy%d_" % layer, 8, [128, NT], F32)
            sg_rot = Rot(sb, "sg%d_" % layer, 2, [128, NT], F32)
            act_rot = Rot(sb, "actc%d_" % layer, 33, [128, NT], BF16)
            mt = None if maxtiles is None else maxtiles + (nphase - (2 if layer == 0 else 4))
            tiles = []
            s = s0
            while s + 2 < T and (mt is None or len(tiles) < mt):
                tiles.append((s, min(NT, T - s)))
                s += NT - 2

            def prep(i):
                s, n = tiles[i]
                xt, xb = load_x(src_v, src_key, s, n, xp_rot)
                sq, sqb = prenorm_a(xt, xb, n)
                return dict(s=s, n=n, xt=xt, xb=xb, sq=sq, sqb=sqb, acts=[], pend=None)

            def prep_b(st):
                st['h'], st['hb'] = prenorm_b(st['xt'], st['xb'], st['sq'], st['sqb'], st['n'], SC_GAIN + (2 * 2 + layer) * 8)
                return st

            def finish(st):
                n = st['n']
                yg, ygb, yv, yvb = st['pend']
                sg, sgb = sg_rot.next()
                ACT(sg[:, 2:n], yg[:, 2:n], AF.Silu, [ygb], [sgb])
                at, atb = act_rot.next()
                TT("dve", at[:, 2:n], sg[:, 2:n], yv[:, 2:n], ALU.mult, [sgb, yvb], [atb])
                st['acts'].append((at, atb))
                st['pend'] = None

            def pairs(st, j0, j1):
                n, h, hb = st['n'], st['h'], st['hb']
                for j in range(j0, j1):
                    bk = palloc()
                    for hf, ch in enumerate((j, 22 + j)):
                        for c in range(8):
                            MM(psum[bk][:, hf * 256:hf * 256 + n],
                               WA[:, WUP + c * 2 * DFF + ch * 128: WUP + c * 2 * DFF + (ch + 1) * 128],
                               h[:, c, 0:n], c == 0, c == 7, [hb, up_b[j // 2]], [pbufs[bk]])
                    ub, ubb = ub_rot.next()
                    ACT(ub[:, :, 0:n], pv3(bk)[:, :, 0:n], AF.Copy, [pbufs[bk]], [ubb])
                    yas = []
                    for hf, ch in enumerate((j, 22 + j)):
                        wcol = SC_FFC + (layer * 44 + ch) * 3
                        ya, yab = ya_rot.next()
                        ACT(ya[:, 2:n], psum[bk][:, hf * 256 + 2:hf * 256 + n], AF.Identity, [pbufs[bk], b_small], [yab],
                            scale=sc(wcol + 2))
                        yas.append((ya, yab))
                    pfree(bk)
                    ys = []
                    for hf, ch in enumerate((j, 22 + j)):
                        wcol = SC_FFC + (layer * 44 + ch) * 3
                        ya, yab = yas[hf]
                        y1, y1b = yy_rot.next()
                        STT(y1[:, 2:n], ub[:, hf, 1:n - 1], sc(wcol + 1), ya[:, 2:n], ALU.mult, ALU.add, [ubb, yab, b_small], [y1b])
                        y2, y2b = yy_rot.next()
                        STT(y2[:, 2:n], ub[:, hf, 0:n - 2], sc(wcol + 0), y1[:, 2:n], ALU.mult, ALU.add, [ubb, y1b, b_small], [y2b])
                        ys.append((y2, y2b))
                    if st['pend'] is not None:
                        finish(st)
                    st['pend'] = (ys[0][0], ys[0][1], ys[1][0], ys[1][1])

            def f_unit(i, st, j):
                pairs(st, j, j + 1)

            def f_unit_end(i, st):
                finish(st)

            def f_bslices(i, st):
                s, n, acts = st['s'], st['n'], st['acts']
                st['xr'] = load_x(src_v, src_key, s, n, xr_rot)
                kch = [(acts[k][0][:, 2:n], acts[k][1], dn_b[k]) for k in range(22)]
                banks, sl = out_proj_slices(kch, lambda k, m: WA[:, WDN + k * D + m * 128: WDN + k * D + (m + 1) * 128], n, 2)
                st['banks'] = banks
                return sl

            def f_bfinal_a(i, st):
                st['nsq'] = norm_residual_a(st['banks'], st['n'], 2)

            def f_bfinal_b(i, st):
                s, n = st['s'], st['n']
                xt, xb = st['xr']
                sq, sqb = st['nsq']
                norm_residual_b(st['banks'], sq, sqb, xt, xb, n, 2, SC_GAIN + (3 * 2 + layer) * 8, s)
                store_x(xt, xb, dst_v, dst_key, s, n, s + 2, dst_off)

            sched = {j: 1 for j in range(2, 20)}
            for j in (16, 17, 18, 19):
                sched[j] = 2
            run_pipeline2(len(tiles), prep, prep_b, 22, f_unit, f_unit_end, f_bslices, f_bfinal_a, f_bfinal_b, sched, 7, 10, 19)
            ph.close()
            set_fence()

        if nphase >= 2:
            ffn_phase(0, xsA_v, "xsA", xsB_v, "xsB", HALO - 6, 0)

        if nphase >= 3:
            ph3 = es.enter_context(ExitStack())
            scope[0] = ph3
            W3IN, W3OUT = 0, 8 * 3 * D
            WA3 = sb("WA3", [128, W3OUT + 8 * D], BF16)
            w3in_b = load_weight_rows(WA3, w3in_d, W3IN, 8, 3 * D, "w3in")
            w3out_b = load_weight_rows(WA3, w3out_d, W3OUT, 8, D, "w3out")
            usb_rot = Rot(sb, "usb", 6, [128, NT], F32)
            cu_rot = Rot(sb, "cu", 8, [128, NT], F32)
            yy3_rot = Rot(sb, "yy3_", 18, [128, NT], F32)
            yb_rot = Rot(sb, "ybm", 16, [128, NT], BF16)
            mt = None if maxtiles is None else maxtiles + (nphase - 3)
            tiles = []
            s = HALO - 4
            while s + 2 < T and (mt is None or len(tiles) < mt):
                tiles.append((s, min(NT, T - s)))
                s += NT - 2

            def prep3(i):
                s, n = tiles[i]
                xt, xb = load_x(xsB_v, "xsB", s, n, xp_rot)
                sq, sqb = prenorm_a(xt, xb, n)
                return dict(s=s, n=n, xt=xt, xb=xb, sq=sq, sqb=sqb, ybs=[])

            def prep3_b(st):
                st['h'], st['hb'] = prenorm_b(st['xt'], st['xb'], st['sq'], st['sqb'], st['n'], SC_GAIN + (0 * 2 + 1) * 8)
                return st

            def chunks3(st, c0, c1, tick):
                n, h, hb = st['n'], st['h'], st['hb']

                def proj3(bk, half, off):
                    for c in range(8):
                        MM(psum[bk][:, half * 256:half * 256 + n], WA3[:, W3IN + c * 3 * D + off: W3IN + c * 3 * D + off + 128],
                           h[:, c, 0:n], c == 0, c == 7, [hb, w3in_b[c]], [pbufs[bk]])

                for c in range(c0, c1):
                    bk = palloc()
                    proj3(bk, 0, 2 * D + c * 128)
                    proj3(bk, 1, D + c * 128)
                    usb, usbb = usb_rot.next()
                    ACT(usb[:, 0:n], psum[bk][:, 0:n], AF.Copy, [pbufs[bk]], [usbb])
                    cu, cub = cu_rot.next()
                    TT("dve", cu[:, 0:n], psum[bk][:, 256:256 + n], usb[:, 0:n], ALU.mult, [pbufs[bk], usbb], [cub])
                    pfree(bk)
                    tick()
                    bk = palloc()
                    proj3(bk, 0, c * 128)
                    wcol = SC_ODC + c * 3
                    y0, y0b = yy3_rot.next()
                    ACT(y0[:, 2:n], cu[:, 0:n - 2], AF.Identity, [cub, b_small], [y0b], scale=sc(wcol + 0))
                    y1, y1b = yy3_rot.next()
                    STT(y1[:, 2:n], cu[:, 1:n - 1], sc(wcol + 1), y0[:, 2:n], ALU.mult, ALU.add, [cub, y0b, b_small], [y1b])
                    y2, y2b = yy3_rot.next()
                    STT(y2[:, 2:n], cu[:, 2:n], sc(wcol + 2), y1[:, 2:n], ALU.mult, ALU.add, [cub, y1b, b_small], [y2b])
                    yb_, ybb = yb_rot.next()
                    TT("dve", yb_[:, 2:n], psum[bk][:, 2:n], y2[:, 2:n], ALU.mult, [pbufs[bk], y2b], [ybb])
                    pfree(bk)
                    st['ybs'].append((yb_, ybb))
                    tick()

            def p3_bslices(i, st):
                s, n, ybs = st['s'], st['n'], st['ybs']
                st['xr'] = load_x(xsB_v, "xsB", s, n, xr_rot)
                kch = [(ybs[k][0][:, 2:n], ybs[k][1], w3out_b[k]) for k in range(8)]
                banks, sl = out_proj_slices(kch, lambda k, m: WA3[:, W3OUT + k * D + m * 128: W3OUT + k * D + (m + 1) * 128], n, 2)
                st['banks'] = banks
                return sl

            def p3_bfinal_a(i, st):
                st['nsq'] = norm_residual_a(st['banks'], st['n'], 2)

            def p3_bfinal_b(i, st):
                s, n = st['s'], st['n']
                xt, xb = st['xr']
                sq, sqb = st['nsq']
                norm_residual_b(st['banks'], sq, sqb, xt, xb, n, 2, SC_GAIN + (1 * 2 + 1) * 8, s)
                store_x(xt, xb, xsA_v, "xsA", s, n, s + 2)

            def p3_A2(i, st, mid):
                mid()
                chunks3(st, 4, 8, lambda: None)

            run_pipeline3(len(tiles), prep3, prep3_b, lambda i, st, tick: chunks3(st, 0, 4, tick), p3_A2,
                          p3_bslices, p3_bfinal_a, p3_bfinal_b)
            ph3.close()
            set_fence()

        if nphase >= 4:
            ffn_phase(1, xsA_v, "xsA", outT_v, None, HALO - 2, HALO)

        S.emit(nc)
    return nc


def _chunk_cols(v):
    return np.ascontiguousarray(v.reshape(-1, 128).T)


def _rot_cols(w):
    r = w.copy()
    nh = w.shape[1] // 64
    for hd in range(nh):
        r[:, hd * 64:hd * 64 + 8] = w[:, hd * 64 + 8:hd * 64 + 16]
        r[:, hd * 64 + 8:hd * 64 + 16] = w[:, hd * 64:hd * 64 + 8]
    return r


_NC_CACHE = {}


def kernel(x, positions, mix_norm_pre, mix_norm_post, ffn_norm_pre, ffn_norm_post,
           ev_w_in, ev_a_conv_w, ev_a_conv_b, ev_a_ln_g, ev_a_ln_b, ev_sinks, ev_w_out,
           od_w_in, od_conv_w, od_w_out, ffn_w_up, ffn_conv_w, ffn_w_down):
    f32 = np.float32
    x = np.asarray(x, f32)
    positions = np.asarray(positions, np.int32)
    w_in = np.asarray(ev_w_in, f32)[0]
    q = w_in[:, 1024:1536]
    k = w_in[:, 1536:1664]
    v = w_in[:, 1664:1792]
    k0, k1 = k[:, 0:64], k[:, 64:128]
    k0r, k1r = _rot_cols(k0), _rot_cols(k1)
    w1in = np.ascontiguousarray(np.concatenate(
        [w_in[:, 0:1024], q, _rot_cols(q), k0, k0, k1, k1, k0r, k0r, k1r, k1r, v], axis=1))
    assert w1in.shape == (D, WIN_EXT)
    w1out = np.ascontiguousarray(np.asarray(ev_w_out, f32)[0])
    w3in = np.ascontiguousarray(np.asarray(od_w_in, f32)[0])
    w3out = np.ascontiguousarray(np.asarray(od_w_out, f32)[0])
    wup = np.ascontiguousarray(np.asarray(ffn_w_up, f32))
    wdn = np.ascontiguousarray(np.asarray(ffn_w_down, f32))
    small = np.zeros((128, SC_N), f32)
    gains = [mix_norm_pre, mix_norm_post, ffn_norm_pre, ffn_norm_post]
    for kind in range(4):
        for layer in range(2):
            c0 = SC_GAIN + (kind * 2 + layer) * 8
            small[:, c0:c0 + 8] = _chunk_cols(np.asarray(gains[kind], f32)[layer])
    c31 = np.asarray(ev_a_conv_w, f32)[0]
    for c in range(4):
        small[:, SC_C31 + c * 31: SC_C31 + (c + 1) * 31] = c31[:, c * 128:(c + 1) * 128].T
    small[:, SC_CB:SC_CB + 4] = _chunk_cols(np.asarray(ev_a_conv_b, f32)[0])
    small[:, SC_LNG:SC_LNG + 4] = _chunk_cols(np.asarray(ev_a_ln_g, f32)[0])
    small[:, SC_LNB:SC_LNB + 4] = _chunk_cols(np.asarray(ev_a_ln_b, f32)[0])
    sinks = np.asarray(ev_sinks, f32)[0]
    for cc in range(4):
        small[0:64, SC_SINK + cc] = sinks[2 * cc]
        small[64:128, SC_SINK + cc] = sinks[2 * cc + 1]
    odc = np.asarray(od_conv_w, f32)[0]
    for c in range(8):
        small[:, SC_ODC + c * 3: SC_ODC + c * 3 + 3] = odc[:, c * 128:(c + 1) * 128].T
    ffc = np.asarray(ffn_conv_w, f32)
    for layer in range(2):
        for c in range(NFF):
            c0 = SC_FFC + (layer * NFF + c) * 3
            small[:, c0:c0 + 3] = ffc[layer][:, c * 128:(c + 1) * 128].T
    inv_freq = (f32(500000.0) ** (-(np.arange(8, dtype=f32) * f32(2.0) / f32(16.0)))).astype(f32)
    for p in range(128):
        d = p % 64
        if d < 8:
            small[p, SC_FS] = -inv_freq[d]
        elif d < 16:
            small[p, SC_FS] = inv_freq[d - 8]
    small[:, SC_EPS_RMS] = 1e-6
    small[:, SC_EPS_LN] = 1e-5
    small[:, SC_HALFPI] = np.pi / 2
    kk = np.arange(128)[:, None]
    qq = np.arange(128)[None, :]
    m_d = np.where(kk <= qq, 0.0, NEG).astype(f32)
    m_p = np.where(kk > qq, 0.0, NEG).astype(f32)
    m_all = np.full((128, 128), NEG, f32)
    in_maps = []
    for core in range(NCORES):
        b, j = core // 4, core % 4
        start = j * TOK
        xT = np.zeros((D, T), f32)
        pos = np.zeros((T,), np.int32)
        lo = start - HALO
        if lo >= 0:
            xT[:] = x[b, lo:start + TOK, :].T
            pos[:] = positions[b, lo:start + TOK]
        else:
            xT[:, HALO:] = x[b, 0:TOK, :].T
            pos[HALO:] = positions[b, 0:TOK]
        sm = small.copy()
        sm[:, SC_HM] = 0.0 if j == 0 else 1.0
        m_f = m_all if j == 0 else m_p
        cst = np.concatenate([np.eye(128, dtype=f32), m_p, m_p, m_d, m_d, m_f, m_f, m_d, m_d], axis=1)
        in_maps.append({
            "xT": np.ascontiguousarray(xT),
            "posr": np.ascontiguousarray(np.broadcast_to(pos[None, :], (128, T))),
            "small": sm,
            "cst": np.ascontiguousarray(cst),
            "w1in": w1in, "w1out": w1out, "w3in": w3in, "w3out": w3out, "wup": wup, "wdn": wdn,
        })
    if _NC_CACHE.get("prep_only"):
        return in_maps
    if "nc" not in _NC_CACHE:
        _NC_CACHE["nc"] = build_program()
    res = run_bass_kernel_spmd(_NC_CACHE["nc"], in_maps, core_ids=list(range(NCORES)))
    out = np.empty((2, SEQ, D), f32)
    for core in range(NCORES):
        b, j = core // 4, core % 4
        out[b, j * TOK:(j + 1) * TOK, :] = res.results[core]["outT"].T
    return out
```

```python
import numpy as np
from contextlib import ExitStack
import concourse.bass as bass
import concourse.mybir as mybir
from concourse.bass_utils import run_bass_kernel_spmd

F32 = mybir.dt.float32
BF16 = mybir.dt.bfloat16
I32 = mybir.dt.int32
ALU = mybir.AluOpType
AF = mybir.ActivationFunctionType

NCORES = 8
D = 1024
SEQ = 16384
TOK = 4096
HALO = 256
T = TOK + HALO
DFF = 2816
NFF = 2 * DFF // 128
WIN_EXT = 2688
NEG = -30000.0
TWO_PI = 6.283185307179586
C1 = 6.28125
C2 = TWO_PI - C1

SC_GAIN = 0
SC_C31 = 64
SC_CB = 188
SC_LNG = 192
SC_LNB = 196
SC_SINK = 200
SC_ODC = 204
SC_FFC = 228
SC_FS = 492
SC_HM = 493
SC_EPS_RMS = 494
SC_EPS_LN = 495
SC_HALFPI = 496
SC_N = 500


class Buf:
    __slots__ = ("name", "w", "r", "serial")

    def __init__(self, name, serial=False):
        self.name = name
        self.w = None
        self.r = []
        self.serial = serial


class Op:
    __slots__ = ("eng", "fn", "deps", "ms", "is_dma", "dsem", "dcount", "used")


class Sched:
    ENGS = ("pe", "act", "dve", "pool", "sp")

    def __init__(self, n_dma_sems=12):
        self.q = {e: [] for e in self.ENGS}
        self.n_dma_sems = n_dma_sems
        self.dma_rr = {e: 0 for e in self.ENGS}
        self.dma_last = {}

    def add(self, eng, fn, reads=(), writes=(), dma=False, extra=()):
        op = Op()
        op.eng = eng
        op.fn = fn
        op.is_dma = dma
        op.ms = 0
        op.used = False
        op.dsem = None
        op.dcount = 0
        deps = list(extra)
        ser = [b for b in reads if b.serial]
        if ser:
            reads = [b for b in reads if not b.serial]
            writes = list(writes) + ser
        for b in reads:
            if b.w is not None:
                deps.append(b.w)
        for b in writes:
            if b.w is not None:
                deps.append(b.w)
            deps.extend(b.r)
        if dma:
            slot = self.dma_rr[eng]
            self.dma_rr[eng] = (slot + 1) % self.n_dma_sems
            prev = self.dma_last.get((eng, slot))
            if prev is not None:
                deps.append(prev)
                op.dcount = prev.dcount + 16
            else:
                op.dcount = 16
            op.dsem = (eng, slot)
            self.dma_last[(eng, slot)] = op
        for b in reads:
            b.r.append(op)
        for b in writes:
            b.w = op
            b.r = []
        op.deps = [d for d in set(deps)
                   if d is not op and not (eng == "pe" and d.eng == "pe" and not d.is_dma and not dma)]
        for d in op.deps:
            d.used = True
        self.q[eng].append(op)
        return op

    def emit(self, nc):
        for e in self.ENGS:
            n = 0
            for op in self.q[e]:
                if op.used and not op.is_dma:
                    n += 1
                    op.ms = n
            assert n < 60000, (e, n)
        with ExitStack() as es:
            csem = {e: es.enter_context(nc.semaphore("c_" + e)) for e in ("pe", "act", "dve", "pool")}
            dsem = {}
            for key in self.dma_last:
                dsem[key] = es.enter_context(nc.semaphore("d_%s%d" % key))
            block = es.enter_context(nc.Block())
            q = self.q
            dma_last = self.dma_last

            def run(e, eng):
                waited = {}
                for op in q[e]:
                    need = {}
                    for d in op.deps:
                        if d.is_dma:
                            k, v = ("d",) + d.dsem, d.dcount
                        else:
                            k, v = ("c", d.eng), d.ms
                        if need.get(k, 0) < v:
                            need[k] = v
                    for k, v in need.items():
                        if waited.get(k, 0) >= v:
                            continue
                        waited[k] = v
                        s = csem[k[1]] if k[0] == "c" else dsem[(k[1], k[2])]
                        eng.wait_ge(s, v)
                    ins = op.fn(eng)
                    if op.is_dma:
                        ins.then_inc(dsem[op.dsem], 16)
                    elif op.used:
                        ins.then_inc(csem[e], 1)
                if e == "sp":
                    for key, op in dma_last.items():
                        if waited.get(("d",) + key, 0) < op.dcount:
                            eng.wait_ge(dsem[key], op.dcount)

            @block.tensor
            def _(eng):
                run("pe", eng)

            @block.scalar
            def _(eng):
                run("act", eng)

            @block.vector
            def _(eng):
                run("dve", eng)

            @block.gpsimd
            def _(eng):
                run("pool", eng)

            @block.sync
            def _(eng):
                run("sp", eng)


class Rot:
    mk = Buf

    def __init__(self, alloc, name, k, shape, dt):
        self.items = [(alloc(name + str(i), shape, dt), Rot.mk(name + str(i))) for i in range(k)]
        self.i = 0

    def next(self):
        it = self.items[self.i]
        self.i = (self.i + 1) % len(self.items)
        return it


class _Stop(Exception):
    pass


def build_program(debug=False, nphase=4, maxtiles=None, stage=99):
    nc = bass.Bass("TRN2", target_bir_lowering=False)
    S = Sched()

    def din(name, shape, dt=F32):
        return nc.dram_tensor(name, list(shape), dt, kind="ExternalInput").ap()

    xT = din("xT", [D, T])
    posr = din("posr", [128, T], I32)
    small_d = din("small", [128, SC_N])
    cst_d = din("cst", [128, 128 + 2 * 512])
    w1in_d = din("w1in", [D, WIN_EXT])
    w1out_d = din("w1out", [D, D])
    w3in_d = din("w3in", [D, 3 * D])
    w3out_d = din("w3out", [D, D])
    wup_d = din("wup", [2, D, 2 * DFF])
    wdn_d = din("wdn", [2, DFF, D])
    outT = nc.dram_tensor("outT", [D, TOK], F32, kind="ExternalOutput").ap()
    kind = {"kind": "ExternalOutput"} if debug else {}
    xsA = nc.dram_tensor("xsA", [D, T], F32, **kind).ap()
    xsB = nc.dram_tensor("xsB", [D, T], F32, **kind).ap()

    def tview(ap):
        return ap.rearrange("(c p) t -> p c t", p=128)

    xT_v, xsA_v, xsB_v, outT_v = tview(xT), tview(xsA), tview(xsB), tview(outT)
    dbufs = {"xsA": [Buf("xsA%d" % i) for i in range(T // 256 + 1)],
             "xsB": [Buf("xsB%d" % i) for i in range(T // 256 + 1)]}

    def dblocks(key, s, n):
        return [dbufs[key][i] for i in range(s // 256, (s + n - 1) // 256 + 1)]

    fence = []

    def mkbuf(name):
        b = Buf(name)
        b.r = list(fence)
        return b

    def set_fence():
        fence[:] = []
        for e in ("pe", "act", "dve", "pool"):
            for op in reversed(S.q[e]):
                if not op.is_dma:
                    fence.append(op)
                    break

    def chk(k):
        if stage < k:
            raise _Stop()

    with ExitStack() as es:
        scope = [es]

        def sb(name, shape, dt):
            return scope[0].enter_context(nc.sbuf_tensor(name, list(shape), dt))

        Rot.mk = staticmethod(mkbuf)
        small = sb("smallt", [128, SC_N], F32)
        cst = sb("cstt", [128, 128 + 2 * 512], BF16)
        ones = sb("ones", [128, 128], BF16)
        olo = sb("olo", [128, 128], BF16)
        ohi = sb("ohi", [128, 128], BF16)
        esink = sb("esink", [128, 4], F32)
        b_small, b_cst, b_ones, b_olo, b_ohi, b_esink = (Buf(n) for n in ("small", "cst", "ones", "olo", "ohi", "esink"))

        psum = [es.enter_context(nc.psum_tensor("ps%d" % i, [128, 512], F32)) for i in range(8)]
        pbufs = [Buf("bank%d" % i, serial=True) for i in range(8)]
        pbusy = [False] * 8
        pfreeq = list(range(8))

        def palloc():
            if not pfreeq:
                raise RuntimeError("out of PSUM banks")
            i = pfreeq.pop(0)
            pbusy[i] = True
            return i

        def pfree(i):
            assert pbusy[i]
            pbusy[i] = False
            pfreeq.append(i)

        def sc(col, n=1):
            return small[:, col:col + n]

        def pv3(bk):
            return psum[bk][:, :].rearrange("p (a b) -> p a b", a=2)

        ident = cst[:, 0:128]
        mask_a = cst[:, 128:640]
        mask_af = cst[:, 640:1152]

        def DMA(q, out, in_, reads=(), writes=()):
            return S.add(q, lambda e, o=out, i=in_: e.dma_start(out=o, in_=i), reads, writes, dma=True)

        def MM(out, lhsT, rhs, start, stop, reads, writes, sgc=False):
            if sgc:
                return S.add("pe", lambda e, o=out, l=lhsT, r=rhs, a=start, b=stop:
                             e.matmul(o, lhsT=l, rhs=r, start=a, stop=b, skip_group_check=True), reads, writes)
            return S.add("pe", lambda e, o=out, l=lhsT, r=rhs, a=start, b=stop: e.matmul(o, lhsT=l, rhs=r, start=a, stop=b),
                         reads, writes)

        def ACT(out, in_, func, reads, writes, bias=None, scale=None):
            def fn(e, o=out, i=in_, f=func, b=bias, s=scale):
                kw = {}
                if b is not None:
                    kw["bias"] = b
                if s is not None:
                    kw["scale"] = s
                return e.activation(out=o, in_=i, func=f, **kw)
            return S.add("act", fn, reads, writes)

        def TT(eng, out, in0, in1, op, reads, writes):
            return S.add(eng, lambda e, o=out, a=in0, b=in1, p=op: e.tensor_tensor(out=o, in0=a, in1=b, op=p), reads, writes)

        def TS(eng, out, in0, s1, op0, reads, writes, s2=None, op1=None):
            def fn(e, o=out, a=in0, x1=s1, x2=s2, p0=op0, p1=op1):
                if p1 is None:
                    return e.tensor_scalar(out=o, in0=a, scalar1=x1, scalar2=None, op0=p0)
                return e.tensor_scalar(out=o, in0=a, scalar1=x1, scalar2=x2, op0=p0, op1=p1)
            return S.add(eng, fn, reads, writes)

        def STT(out, in0, scalar, in1, op0, op1, reads, writes):
            return S.add("dve", lambda e, o=out, a=in0, s=scalar, b=in1, p0=op0, p1=op1:
                         e.scalar_tensor_tensor(out=o, in0=a, scalar=s, in1=b, op0=p0, op1=p1), reads, writes)

        def COPY(eng, out, in_, reads, writes):
            if eng == "act":
                return S.add(eng, lambda e, o=out, i=in_: e.activation(out=o, in_=i, func=AF.Copy), reads, writes)
            return S.add(eng, lambda e, o=out, i=in_: e.tensor_copy(out=o, in_=i), reads, writes)

        def RECIP(out, in_, reads, writes):
            return S.add("dve", lambda e, o=out, i=in_: e.reciprocal(out=o, in_=i), reads, writes)

        def MEMSET(eng, ap, val, writes):
            return S.add(eng, lambda e, a=ap, v=val: e.memset(a, v), (), writes)

        DMA("sp", small[:], small_d, writes=[b_small])
        DMA("pool", cst[:], cst_d, writes=[b_cst])
        MEMSET("pool", ones[:], 1.0, [b_ones])
        MEMSET("pool", olo[:], 0.0, [b_olo])
        MEMSET("pool", olo[:, 0:64], 1.0, [b_olo])
        MEMSET("pool", ohi[:], 0.0, [b_ohi])
        MEMSET("pool", ohi[:, 64:128], 1.0, [b_ohi])
        ACT(esink[:], sc(SC_SINK, 4), AF.Exp, [b_small], [b_esink])

        NT = 256
        xp_rot = Rot(sb, "xp", 1, [128, 8, NT], F32)
        xr_rot = Rot(sb, "xr", 1, [128, 8, NT], F32)
        sq_rot = Rot(sb, "sq", 2, [128, 8, NT], BF16)
        h_rot = Rot(sb, "h", 2, [128, 8, NT], BF16)
        stat_rot = Rot(sb, "stat", 3, [128, NT], F32)
        tmp_rot = Rot(sb, "tmp", 2, [128, NT], F32)

        def load_weight_rows(WA, dram2d, col0, nchunks, ncols, tag):
            bufs = []
            for c in range(nchunks):
                b = mkbuf("%s%d" % (tag, c))
                DMA("pool", WA[:, col0 + c * ncols: col0 + (c + 1) * ncols], dram2d[c * 128:(c + 1) * 128, :],
                    writes=[b])
                bufs.append(b)
            return bufs

        def rms_rstd(sq_chunks, n, reads):
            bk = palloc()
            nchk = len(sq_chunks)
            for c, ap in enumerate(sq_chunks):
                MM(psum[bk][:, 0:n], ones[:], ap, c == 0, c == nchk - 1, list(reads) + [b_ones], [pbufs[bk]])
            sd, sdb = stat_rot.next()
            ACT(sd[:, 0:n], psum[bk][:, 0:n], AF.Sqrt, [pbufs[bk], b_small], [sdb], bias=sc(SC_EPS_RMS), scale=1.0 / D)
            pfree(bk)
            rs, rsb = stat_rot.next()
            RECIP(rs[:, 0:n], sd[:, 0:n], [sdb], [rsb])
            return rs, rsb

        def load_x(src_v, src_key, s, n, pool):
            xt, xb = pool.next()
            rd = dblocks(src_key, s, n) if src_key else []
            DMA("sp", xt[:, :, 0:n], src_v[:, :, s:s + n], reads=rd, writes=[xb])
            return xt, xb

        def prenorm_a(xt, xb, n):
            sq, sqb = sq_rot.next()
            ACT(sq[:, :, 0:n], xt[:, :, 0:n], AF.Square, [xb], [sqb])
            return sq, sqb

        def prenorm_b(xt, xb, sq, sqb, n, gcol):
            rs, rsb = rms_rstd([sq[:, c, 0:n] for c in range(8)], n, [sqb])
            h, hb = h_rot.next()
            for c in range(8):
                STT(h[:, c, 0:n], xt[:, c, 0:n], sc(gcol + c), rs[:, 0:n], ALU.mult, ALU.mult, [xb, rsb, b_small], [hb])
            return h, hb

        def out_proj_slices(kchunks, wfn, n, c0, order=None):
            w = n - c0
            nk = len(kchunks)
            banks = []

            order = list(range(nk)) if order is None else order

            def mk(pos, k):
                def fn():
                    if pos == 0:
                        for mp in range(4):
                            banks.append(palloc())
                    ap, kb, wb = kchunks[k]
                    for mp in range(4):
                        bk = banks[mp]
                        for hf in range(2):
                            MM(psum[bk][:, hf * 256:hf * 256 + w], wfn(k, 2 * mp + hf), ap, pos == 0 and hf == 0, pos == nk - 1,
                               [kb, wb], [pbufs[bk]], sgc=True)
                return fn
            return banks, [mk(pos, k) for pos, k in enumerate(order)]

        def norm_residual(banks, xt, xb, n, c0, gcol, s):
            w = n - c0
            sq, sqb = norm_residual_a(banks, n, c0)
            norm_residual_b(banks, sq, sqb, xt, xb, n, c0, gcol, s)

        def norm_residual_a(banks, n, c0):
            w = n - c0
            sq, sqb = sq_rot.next()
            for mp, bk in enumerate(banks):
                ACT(sq[:, 2 * mp:2 * mp + 2, 0:w], pv3(bk)[:, :, 0:w], AF.Square, [pbufs[bk]], [sqb])
            return sq, sqb

        def norm_residual_b(banks, sq, sqb, xt, xb, n, c0, gcol, s):
            w = n - c0
            rs, rsb = rms_rstd([sq[:, c, 0:w] for c in range(8)], w, [sqb])
            for mp, bk in enumerate(banks):
                for hf in range(2):
                    c = 2 * mp + hf
                    tm, tmb = tmp_rot.next()
                    STT(tm[:, 0:w], psum[bk][:, hf * 256:hf * 256 + w], sc(gcol + c), rs[:, 0:w], ALU.mult, ALU.mult,
                        [pbufs[bk], rsb, b_small], [tmb])
                    TT("dve", xt[:, c, c0:n], tm[:, 0:w], xt[:, c, c0:n], ALU.add, [tmb, xb], [xb])
                pfree(bk)
            if s < HALO:
                hc = min(HALO - s, n)
                TS("dve", xt[:, :, 0:hc], xt[:, :, 0:hc], sc(SC_HM), ALU.mult, [xb, b_small], [xb])

        def out_proj_norm_residual(kchunks, wfn, xt, xb, n, c0, gcol, s):
            banks, sl = out_proj_slices(kchunks, wfn, n, c0)
            for f in sl:
                f()
            norm_residual(banks, xt, xb, n, c0, gcol, s)

        def store_x(xt, xb, dst_v, dst_key, s, n, wlo, dst_off=0):
            wr = dblocks(dst_key, wlo, s + n - wlo) if dst_key else []
            DMA("sp", dst_v[:, :, wlo - dst_off:s + n - dst_off], xt[:, :, wlo - s:n], reads=[xb], writes=wr)

        def run_pipeline(nt, prep, A1, A2, Bst):
            if nt == 0:
                return
            st = {0: prep(0)}
            A1(0, st[0])
            if nt > 1:
                st[1] = prep(1)
            A2(0, st[0])
            for i in range(nt):
                if i + 1 < nt:
                    A1(i + 1, st[i + 1])
                if i + 2 < nt:
                    st[i + 2] = prep(i + 2)
                Bst(i, st[i])
                del st[i]
                if i + 1 < nt:
                    A2(i + 1, st[i + 1])

        def run_pipeline2(nt, prep_a, prep_b, nunits, unit, unit_end, bslices, bfinal_a, bfinal_b, sched, pa_at, pb_at, fa_at):
            if nt == 0:
                return
            st = {0: prep_b(prep_a(0))}
            for j in range(nunits):
                unit(0, st[0], j)
                if j == pa_at and nt > 1:
                    st[1] = prep_a(1)
                if j == pb_at and nt > 1:
                    st[1] = prep_b(st[1])
            unit_end(0, st[0])
            for i in range(nt):
                pending = bslices(i, st[i])
                idx = 0
                if i + 1 < nt:
                    for j in range(nunits):
                        unit(i + 1, st[i + 1], j)
                        for _ in range(sched.get(j, 0)):
                            if idx < len(pending):
                                pending[idx]()
                                idx += 1
                        if j == fa_at:
                            while idx < len(pending):
                                pending[idx]()
                                idx += 1
                            bfinal_a(i, st[i])
                        if j == pa_at and i + 2 < nt:
                            st[i + 2] = prep_a(i + 2)
                        if j == pb_at and i + 2 < nt:
                            st[i + 2] = prep_b(st[i + 2])
                    unit_end(i + 1, st[i + 1])
                else:
                    while idx < len(pending):
                        pending[idx]()
                        idx += 1
                    bfinal_a(i, st[i])
                bfinal_b(i, st[i])
                del st[i]

        def run_pipeline3(nt, prep_a, prep_b, A1, A2, bslices, bfinal_a, bfinal_b, prep_tick=0):
            if nt == 0:
                return

            def noop():
                pass

            st = {0: prep_b(prep_a(0))}
            A1(0, st[0], noop)
            if nt > 1:
                st[1] = prep_b(prep_a(1))
            A2(0, st[0], noop)
            for i in range(nt):
                pending = bslices(i, st[i])
                state = {'idx': 0, 'calls': 0, 'prepped': False}

                def do_prep(i=i, state=state):
                    if not state['prepped'] and i + 2 < nt:
                        st[i + 2] = prep_a(i + 2)
                    state['prepped'] = True

                def tick(pending=pending, state=state, do_prep=do_prep):
                    if state['calls'] == prep_tick:
                        do_prep()
                    state['calls'] += 1
                    if state['idx'] < len(pending):
                        pending[state['idx']]()
                        state['idx'] += 1

                if prep_tick == 0:
                    do_prep()
                if i + 1 < nt:
                    A1(i + 1, st[i + 1], tick)
                do_prep()
                while state['idx'] < len(pending):
                    pending[state['idx']]()
                    state['idx'] += 1
                bfinal_a(i, st[i])

                def mid(i=i):
                    if i + 2 < nt:
                        st[i + 2] = prep_b(st[i + 2])
                    bfinal_b(i, st[i])

                if i + 1 < nt:
                    A2(i + 1, st[i + 1], mid)
                else:
                    mid()
                del st[i]

        ph1 = es.enter_context(ExitStack())
        scope[0] = ph1
        W1IN, W1OUT, DIAG = 0, 8 * WIN_EXT, 8 * WIN_EXT + 8 * D
        WA1 = sb("WA1", [128, DIAG + 124 * 128], BF16)
        WA1v = WA1[:, W1IN:W1IN + 8 * WIN_EXT].rearrange("p (c m) -> p c m", c=8)
        w1in_v = w1in_d.rearrange("(c p) m -> p c m", p=128)
        win_piece = {}
        for nm, (c0, c1) in (("v", (2560, 2688)), ("k", (2048, 2560)), ("q", (1024, 2048)), ("glu", (0, 1024))):
            b = mkbuf("w1in_" + nm)
            for cc0 in range(c0, c1, 512):
                cc1 = min(cc0 + 512, c1)
                DMA("pool", WA1v[:, :, cc0:cc1], w1in_v[:, :, cc0:cc1], writes=[b])
            win_piece[nm] = b

        def winb(off):
            return win_piece["v" if off >= 2560 else "k" if off >= 2048 else "q" if off >= 1024 else "glu"]
        wout_b = load_weight_rows(WA1, w1out_d, W1OUT, 8, D, "w1out")
        b_diag = mkbuf("diag")
        identh = sb("identh", [128, 128], BF16)
        b_identh = mkbuf("identh")
        TS("dve", identh[:], ident, 0.5, ALU.mult, [b_cst], [b_identh])
        for c in range(4):
            for j in range(31):
                col = DIAG + (c * 31 + j) * 128
                TS("dve", WA1[:, col:col + 128], identh[:], sc(SC_C31 + c * 31 + j), ALU.mult, [b_identh, b_small], [b_diag])

        def w1in(c, off, m):
            return WA1[:, W1IN + c * WIN_EXT + off: W1IN + c * WIN_EXT + off + m]

        NB = NT // 128
        abuf = [sb("abuf%d" % c, [128, 30 + NT], BF16) for c in range(4)]
        b_abuf = [mkbuf("abuf%d" % c) for c in range(4)]
        krot = [sb("krot%d" % g, [128, 128 + NT], BF16) for g in range(2)]
        b_krot = [mkbuf("krot%d" % g) for g in range(2)]
        vlo = [sb("vlo%d" % g, [128, (NB + 1) * 128], BF16) for g in range(2)]
        vhi = [sb("vhi%d" % g, [128, (NB + 1) * 128], BF16) for g in range(2)]
        b_v = [mkbuf("v%d" % g) for g in range(2)]
        for c in range(4):
            MEMSET("pool", abuf[c][:], 0.0, [b_abuf[c]])
        for g in range(2):
            MEMSET("pool", krot[g][:], 0.0, [b_krot[g]])
            MEMSET("pool", vlo[g][:], 0.0, [b_v[g]])
            MEMSET("pool", vhi[g][:], 0.0, [b_v[g]])
        qrot_rot = Rot(sb, "qrot", 8, [128, NT], BF16)
        pT_rot = Rot(sb, "pT", 8, [128, 512], BF16)
        pos_rot = Rot(sb, "posi", 2, [128, NT], I32)
        ki_rot = Rot(sb, "ki", 2, [128, NT], I32)
        cs_rot = Rot(sb, "cs", 4, [128, NT], F32)
        rt_rot = Rot(sb, "rt", 7, [128, NT], F32)
        tg_rot = Rot(sb, "tg", 2, [128, NT], F32)
        ac_rot = Rot(sb, "ac", 4, [128, NT], F32)
        acb_rot = Rot(sb, "acb", 4, [128, NT], BF16)
        sq2_rot = Rot(sb, "sq2", 4, [128, NT], BF16)
        mix_rot = Rot(sb, "mix", 8, [128, NT], BF16)

        ntiles1 = T // NT if maxtiles is None else (maxtiles + (nphase - 1))
        def prep1(ti):
            s, n = ti * NT, NT
            xt, xb = load_x(xT_v, None, s, n, xp_rot)
            pi_, pib = pos_rot.next()
            DMA("sp", pi_[:, 0:n], posr[:, s:s + n], writes=[pib])
            posf, posfb = rt_rot.next()
            COPY("dve", posf[:, 0:n], pi_[:, 0:n], [pib], [posfb])
            ang, angb = rt_rot.next()
            TS("dve", ang[:, 0:n], posf[:, 0:n], sc(SC_FS), ALU.mult, [posfb, b_small], [angb])
            kq, kqb = rt_rot.next()
            TS("dve", kq[:, 0:n], ang[:, 0:n], 1.0 / TWO_PI, ALU.mult, [angb], [kqb])
            ki, kib = ki_rot.next()
            COPY("dve", ki[:, 0:n], kq[:, 0:n], [kqb], [kib])
            kf, kfb = rt_rot.next()
            COPY("dve", kf[:, 0:n], ki[:, 0:n], [kib], [kfb])
            r1, r1b = rt_rot.next()
            STT(r1[:, 0:n], kf[:, 0:n], -C1, ang[:, 0:n], ALU.mult, ALU.add, [kfb, angb], [r1b])
            r2, r2b = rt_rot.next()
            STT(r2[:, 0:n], kf[:, 0:n], -C2, r1[:, 0:n], ALU.mult, ALU.add, [kfb, r1b], [r2b])
            r3, r3b = rt_rot.next()
            TS("dve", r3[:, 0:n], r2[:, 0:n], -3.1415925, ALU.max, [r2b], [r3b], s2=3.1415925, op1=ALU.min)
            Sf, Sfb = cs_rot.next()
            ACT(Sf[:, 0:n], r3[:, 0:n], AF.Sin, [r3b], [Sfb])
            ar, arb = rt_rot.next()
            ACT(ar[:, 0:n], r3[:, 0:n], AF.Abs, [r3b], [arb])
            Cf, Cfb = cs_rot.next()
            ACT(Cf[:, 0:n], ar[:, 0:n], AF.Sin, [arb, b_small], [Cfb], bias=sc(SC_HALFPI), scale=-1.0)

            sq, sqb = prenorm_a(xt, xb, n)
            return dict(s=s, n=n, xt=xt, xb=xb, sq=sq, sqb=sqb, Cf=Cf, Cfb=Cfb, Sf=Sf, Sfb=Sfb)

        def prep1_b(st):
            st['h'], st['hb'] = prenorm_b(st['xt'], st['xb'], st['sq'], st['sqb'], st['n'], SC_GAIN + (0 * 2 + 0) * 8)
            return st

        def p1_A1(ti, st, tick):
            s, n, h, hb = st['s'], st['n'], st['h'], st['hb']
            Cf, Cfb, Sf, Sfb = st['Cf'], st['Cfb'], st['Sf'], st['Sfb']
            def proj_into(bk, half, off):
                for c in range(8):
                    MM(psum[bk][:, half * 256:half * 256 + n], w1in(c, off, 128), h[:, c, 0:n], c == 0, c == 7,
                       [hb, winb(off)], [pbufs[bk]])

            bk = palloc()
            for b in range(NB):
                for c in range(8):
                    MM(psum[bk][:, b * 128:(b + 1) * 128], h[:, c, b * 128:(b + 1) * 128], w1in(c, 2560, 128),
                       c == 0, c == 7, [hb, winb(2560)], [pbufs[bk]])
            for b in range(NB):
                for g in range(2):
                    COPY("act", vlo[g][:, (b + 1) * 128:(b + 1) * 128 + 64],
                         psum[bk][:, b * 128 + g * 64:b * 128 + g * 64 + 64], [pbufs[bk]], [b_v[g]])
                    COPY("act", vhi[g][:, (b + 1) * 128 + 64:(b + 2) * 128],
                         psum[bk][:, b * 128 + g * 64:b * 128 + g * 64 + 64], [pbufs[bk]], [b_v[g]])
            pfree(bk)
            tick()

            def rope(off_x, off_r, out_ap, out_buf):
                bk = palloc()
                proj_into(bk, 0, off_x)
                proj_into(bk, 1, off_r)
                t1, t1b = tmp_rot.next()
                TT("dve", t1[:, 0:n], psum[bk][:, 0:n], Cf[:, 0:n], ALU.mult, [pbufs[bk], Cfb], [t1b])
                t2, t2b = tmp_rot.next()
                TT("dve", t2[:, 0:n], psum[bk][:, 256:256 + n], Sf[:, 0:n], ALU.mult, [pbufs[bk], Sfb], [t2b])
                pfree(bk)
                TT("dve", out_ap, t1[:, 0:n], t2[:, 0:n], ALU.add, [t1b, t2b], [out_buf])

            for g in range(2):
                rope(2048 + g * 128, 2304 + g * 128, krot[g][:, 128:128 + n], b_krot[g])
                tick()
            qr_t = []
            for cq in range(4):
                qt, qb = qrot_rot.next()
                rope(1024 + cq * 128, 1536 + cq * 128, qt[:, 0:n], qb)
                qr_t.append((qt, qb))
                tick()
            for c in range(4):
                bk = palloc()
                proj_into(bk, 0, 512 + c * 128)
                proj_into(bk, 1, c * 128)
                tg, tgb = tg_rot.next()
                ACT(tg[:, 0:n], psum[bk][:, 0:n], AF.Tanh, [pbufs[bk]], [tgb], scale=0.5)
                STT(abuf[c][:, 30:30 + n], tg[:, 0:n], 1.0, psum[bk][:, 256:256 + n], ALU.add, ALU.mult,
                    [tgb, pbufs[bk]], [b_abuf[c]])
                pfree(bk)
                tick()
            pTs = {}
            for g in range(2):
                for b in range(NB):
                    gblk = ti * NB + b
                    for half in range(2):
                        bk = palloc()
                        msk = mask_af if gblk == HALO // 128 else mask_a
                        MM(psum[bk][:, :], ident, msk, True, False, [b_cst], [pbufs[bk]])
                        for kb in range(2):
                            kcol = (b + kb) * 128
                            for ii in range(2):
                                qt, qb = qr_t[2 * g + ii]
                                col = (kb * 2 + ii) * 128
                                MM(psum[bk][:, col:col + 128], krot[g][half * 64:(half + 1) * 64, kcol:kcol + 128],
                                   qt[half * 64:(half + 1) * 64, b * 128:(b + 1) * 128], False, kb == 1 and ii == 1,
                                   [b_krot[g], qb], [pbufs[bk]])
                        p_t, p_b = pT_rot.next()
                        ACT(p_t[:, :], psum[bk][:, :], AF.Exp, [pbufs[bk]], [p_b], scale=0.125)
                        pfree(bk)
                        pTs[(g, b, half)] = (p_t, p_b)
                        tick()
            st['pTs'] = pTs
            st['proj_into'] = proj_into

        def p1_A2(ti, st, tick):
            s, n, h, hb = st['s'], st['n'], st['h'], st['hb']
            Cf, Cfb, Sf, Sfb = st['Cf'], st['Cfb'], st['Sf'], st['Sfb']
            pTs, proj_into = st['pTs'], st['proj_into']
            tick()
            acs = []
            for cp in range(2):
                bk = palloc()
                for hf in range(2):
                    c = 2 * cp + hf
                    for j in range(31):
                        col = DIAG + (c * 31 + j) * 128
                        MM(psum[bk][:, hf * 256:hf * 256 + n], WA1[:, col:col + 128], abuf[c][:, j:j + n], j == 0, j == 30,
                           [b_diag, b_abuf[c]], [pbufs[bk]])
                for hf in range(2):
                    c = 2 * cp + hf
                    src = psum[bk][:, hf * 256:hf * 256 + n]
                    ac, acb_ = ac_rot.next()
                    ACT(ac[:, 0:n], src, AF.Identity, [pbufs[bk], b_small], [acb_], bias=sc(SC_CB + c))
                    s2, s2b = sq2_rot.next()
                    ACT(s2[:, 0:n], src, AF.Square, [pbufs[bk], b_small], [s2b], bias=sc(SC_CB + c))
                    a16, a16b = acb_rot.next()
                    COPY("dve", a16[:, 0:n], ac[:, 0:n], [acb_], [a16b])
                    acs.append((ac, acb_, a16, a16b, s2, s2b))
                    COPY("dve", abuf[c][:, 0:30], abuf[c][:, n:n + 30], [b_abuf[c]], [b_abuf[c]])
                pfree(bk)
            ochunks = []
            for cc in range(4):
                g = cc // 2
                iA = 2 * (cc % 2)
                bk = palloc()
                ii = cc % 2
                for b in range(NB):
                    k = 0
                    for kb in range(2):
                        vcol = (b + kb) * 128
                        col = (kb * 2 + ii) * 128
                        for hh, vt in enumerate((vlo[g], vhi[g])):
                            p_t, p_b = pTs[(g, b, hh)]
                            MM(psum[bk][:, b * 128:(b + 1) * 128], vt[:, vcol:vcol + 128],
                               p_t[:, col:col + 128], k == 0, k == 3, [b_v[g], p_b], [pbufs[bk]])
                            k += 1
                for b in range(NB):
                    k = 0
                    for kb in range(2):
                        col = (kb * 2 + ii) * 128
                        for hh, ot in enumerate((olo, ohi)):
                            p_t, p_b = pTs[(g, b, hh)]
                            MM(psum[bk][:, 256 + b * 128:256 + (b + 1) * 128], ot[:],
                               p_t[:, col:col + 128], k == 0, k == 3, [b_olo, b_ohi, p_b], [pbufs[bk]])
                            k += 1
                dn, dnb = stat_rot.next()
                ACT(dn[:, 0:n], psum[bk][:, 256:256 + n], AF.Identity, [pbufs[bk], b_esink], [dnb], bias=esink[:, cc:cc + 1])
                rd_, rdb = stat_rot.next()
                RECIP(rd_[:, 0:n], dn[:, 0:n], [dnb], [rdb])
                mo, mob = mix_rot.next()
                TT("dve", mo[:, 0:n], psum[bk][:, 0:n], rd_[:, 0:n], ALU.mult, [pbufs[bk], rdb], [mob])
                pfree(bk)
                ochunks.append((mo, mob))
            for g in range(2):
                COPY("dve", krot[g][:, 0:128], krot[g][:, n:n + 128], [b_krot[g]], [b_krot[g]])
                COPY("dve", vlo[g][:, 0:128], vlo[g][:, NB * 128:(NB + 1) * 128], [b_v[g]], [b_v[g]])
                COPY("dve", vhi[g][:, 0:128], vhi[g][:, NB * 128:(NB + 1) * 128], [b_v[g]], [b_v[g]])
            bk = palloc()
            for c in range(4):
                MM(psum[bk][:, 0:n], ones[:], acs[c][2][:, 0:n], c == 0, c == 3, [acs[c][3], b_ones], [pbufs[bk]])
            for c in range(4):
                MM(psum[bk][:, 256:256 + n], ones[:], acs[c][4][:, 0:n], c == 0, c == 3, [acs[c][5], b_ones], [pbufs[bk]])
            mean, meanb = stat_rot.next()
            ACT(mean[:, 0:n], psum[bk][:, 0:n], AF.Copy, [pbufs[bk]], [meanb], scale=1.0 / 512)
            msq, msqb = tmp_rot.next()
            ACT(msq[:, 0:n], psum[bk][:, 0:n], AF.Square, [pbufs[bk]], [msqb], scale=1.0 / 512)
            var, varb = tmp_rot.next()
            STT(var[:, 0:n], psum[bk][:, 256:256 + n], 1.0 / 512, msq[:, 0:n], ALU.mult, ALU.subtract, [pbufs[bk], msqb], [varb])
            pfree(bk)
            sd, sdb = tmp_rot.next()
            ACT(sd[:, 0:n], var[:, 0:n], AF.Sqrt, [varb, b_small], [sdb], bias=sc(SC_EPS_LN))
            rl, rlb = stat_rot.next()
            RECIP(rl[:, 0:n], sd[:, 0:n], [sdb], [rlb])
            achunks = []
            for c in range(4):
                ac, acb_ = acs[c][0], acs[c][1]
                t1, t1b = tmp_rot.next()
                TT("dve", t1[:, 0:n], ac[:, 0:n], mean[:, 0:n], ALU.subtract, [acb_, meanb], [t1b])
                t2, t2b = tmp_rot.next()
                STT(t2[:, 0:n], t1[:, 0:n], sc(SC_LNG + c), rl[:, 0:n], ALU.mult, ALU.mult, [t1b, rlb, b_small], [t2b])
                ma, mab = mix_rot.next()
                ACT(ma[:, 0:n], t2[:, 0:n], AF.Silu, [t2b, b_small], [mab], bias=sc(SC_LNB + c))
                achunks.append((ma, mab))
            st['mixk'] = achunks + ochunks

        def p1_bslices(ti, st):
            s, n = st['s'], st['n']
            st['xr'] = load_x(xT_v, None, s, n, xr_rot)
            mixk = st['mixk']
            kch = [(mixk[k][0][:, 0:n], mixk[k][1], wout_b[k]) for k in range(8)]
            banks, sl = out_proj_slices(kch, lambda k, m: WA1[:, W1OUT + k * D + m * 128: W1OUT + k * D + (m + 1) * 128], n, 0,
                                        order=[4, 5, 6, 7, 0, 1, 2, 3])
            st['banks'] = banks
            return sl

        def p1_bfinal_a(ti, st):
            st['nsq'] = norm_residual_a(st['banks'], st['n'], 0)

        def p1_bfinal_b(ti, st):
            s, n = st['s'], st['n']
            xt, xb = st['xr']
            sq, sqb = st['nsq']
            norm_residual_b(st['banks'], sq, sqb, xt, xb, n, 0, SC_GAIN + (1 * 2 + 0) * 8, s)
            store_x(xt, xb, xsA_v, "xsA", s, n, s)

        run_pipeline3(ntiles1, prep1, prep1_b, p1_A1, p1_A2, p1_bslices, p1_bfinal_a, p1_bfinal_b, prep_tick=7)
        ph1.close()
        set_fence()

        def ffn_phase(layer, src_v, src_key, dst_v, dst_key, s0, dst_off):
            ph = es.enter_context(ExitStack())
            scope[0] = ph
            WUP, WDN = 0, 8 * 2 * DFF
            WA = sb("WAF%d" % layer, [128, WDN + 22 * D], BF16)
            WAv = WA[:, WUP:WUP + 8 * 2 * DFF].rearrange("p (c m) -> p c m", c=8)
            wup_v = wup_d[layer].rearrange("(c p) m -> p c m", p=128)
            up_b = []
            for pc in range(11):
                b = mkbuf("wup%d_p%d" % (layer, pc))
                for base in (0, DFF):
                    c0 = base + pc * 256
                    DMA("pool", WAv[:, :, c0:c0 + 256], wup_v[:, :, c0:c0 + 256], writes=[b])
                up_b.append(b)
            dn_b = load_weight_rows(WA, wdn_d[layer], WDN, 22, D, "wdn%d_" % layer)
            ub_rot = Rot(sb, "ub%d_" % layer, 2, [128, 2, NT], F32)
            ya_rot = Rot(sb, "ya%d_" % layer, 3, [128, NT], F32)
            yy_rot = Rot(sb, "yy%d_" % layer, 8, [128, NT], F32)
            sg_rot = Rot(sb, "sg%d_" % layer, 2, [128, NT], F32)
            act_rot = Rot(sb, "actc%d_" % layer, 33, [128, NT], BF16)
            mt = None if maxtiles is None else maxtiles + (nphase - (2 if layer == 0 else 4))
            tiles = []
            s = s0
            while s + 2 < T and (mt is None or len(tiles) < mt):
                tiles.append((s, min(NT, T - s)))
                s += NT - 2

            def prep(i):
                s, n = tiles[i]
                xt, xb = load_x(src_v, src_key, s, n, xp_rot)
                sq, sqb = prenorm_a(xt, xb, n)
                return dict(s=s, n=n, xt=xt, xb=xb, sq=sq, sqb=sqb, acts=[], pend=None)

            def prep_b(st):
                st['h'], st['hb'] = prenorm_b(st['xt'], st['xb'], st['sq'], st['sqb'], st['n'], SC_GAIN + (2 * 2 + layer) * 8)
                return st

            def finish(st):
                n = st['n']
                yg, ygb, yv, yvb = st['pend']
                sg, sgb = sg_rot.next()
                ACT(sg[:, 2:n], yg[:, 2:n], AF.Silu, [ygb], [sgb])
                at, atb = act_rot.next()
                TT("dve", at[:, 2:n], sg[:, 2:n], yv[:, 2:n], ALU.mult, [sgb, yvb], [atb])
                st['acts'].append((at, atb))
                st['pend'] = None

            def pairs(st, j0, j1):
                n, h, hb = st['n'], st['h'], st['hb']
                for j in range(j0, j1):
                    bk = palloc()
                    for hf, ch in enumerate((j, 22 + j)):
                        for c in range(8):
                            MM(psum[bk][:, hf * 256:hf * 256 + n],
                               WA[:, WUP + c * 2 * DFF + ch * 128: WUP + c * 2 * DFF + (ch + 1) * 128],
                               h[:, c, 0:n], c == 0, c == 7, [hb, up_b[j // 2]], [pbufs[bk]])
                    ub, ubb = ub_rot.next()
                    ACT(ub[:, :, 0:n], pv3(bk)[:, :, 0:n], AF.Copy, [pbufs[bk]], [ubb])
                    yas = []
                    for hf, ch in enumerate((j, 22 + j)):
                        wcol = SC_FFC + (layer * 44 + ch) * 3
                        ya, yab = ya_rot.next()
                        ACT(ya[:, 2:n], psum[bk][:, hf * 256 + 2:hf * 256 + n], AF.Identity, [pbufs[bk], b_small], [yab],
                            scale=sc(wcol + 2))
                        yas.append((ya, yab))
                    pfree(bk)
                    ys = []
                    for hf, ch in enumerate((j, 22 + j)):
                        wcol = SC_FFC + (layer * 44 + ch) * 3
                        ya, yab = yas[hf]
                        y1, y1b = yy_rot.next()
                        STT(y1[:, 2:n], ub[:, hf, 1:n - 1], sc(wcol + 1), ya[:, 2:n], ALU.mult, ALU.add, [ubb, yab, b_small], [y1b])
                        y2, y2b = yy_rot.next()
                        STT(y2[:, 2:n], ub[:, hf, 0:n - 2], sc(wcol + 0), y1[:, 2:n], ALU.mult, ALU.add, [ubb, y1b, b_small], [y2b])
                        ys.append((y2, y2b))
                    if st['pend'] is not None:
                        finish(st)
                    st['pend'] = (ys[0][0], ys[0][1], ys[1][0], ys[1][1])

            def f_unit(i, st, j):
                pairs(st, j, j + 1)

            def f_unit_end(i, st):
                finish(st)

            def f_bslices(i, st):
                s, n, acts = st['s'], st['n'], st['acts']
                st['xr'] = load_x(src_v, src_key, s, n, xr_rot)
                kch = [(acts[k][0][:, 2:n], acts[k][1], dn_b[k]) for k in range(22)]
                banks, sl = out_proj_slices(kch, lambda k, m: WA[:, WDN + k * D + m * 128: WDN + k * D + (m + 1) * 128], n, 2)
                st['banks'] = banks
                return sl

            def f_bfinal_a(i, st):
                st['nsq'] = norm_residual_a(st['banks'], st['n'], 2)

            def f_bfinal_b(i, st):
                s, n = st['s'], st['n']
                xt, xb = st['xr']
                sq, sqb = st['nsq']
                norm_residual_b(st['banks'], sq, sqb, xt, xb, n, 2, SC_GAIN + (3 * 2 + layer) * 8, s)
                store_x(xt, xb, dst_v, dst_key, s, n, s + 2, dst_off)

            sched = {j: 1 for j in range(2, 20)}
            for j in (16, 17, 18, 19):
                sched[j] = 2
            run_pipeline2(len(tiles), prep, prep_b, 22, f_unit, f_unit_end, f_bslices, f_bfinal_a, f_bfinal_b, sched, 7, 10, 19)
            ph.close()
            set_fence()

        if nphase >= 2:
            ffn_phase(0, xsA_v, "xsA", xsB_v, "xsB", HALO - 6, 0)

        if nphase >= 3:
            ph3 = es.enter_context(ExitStack())
            scope[0] = ph3
            W3IN, W3OUT = 0, 8 * 3 * D
            WA3 = sb("WA3", [128, W3OUT + 8 * D], BF16)
            w3in_b = load_weight_rows(WA3, w3in_d, W3IN, 8, 3 * D, "w3in")
            w3out_b = load_weight_rows(WA3, w3out_d, W3OUT, 8, D, "w3out")
            usb_rot = Rot(sb, "usb", 6, [128, NT], F32)
            cu_rot = Rot(sb, "cu", 8, [128, NT], F32)
            yy3_rot = Rot(sb, "yy3_", 18, [128, NT], F32)
            yb_rot = Rot(sb, "ybm", 16, [128, NT], BF16)
            mt = None if maxtiles is None else maxtiles + (nphase - 3)
            tiles = []
            s = HALO - 4
            while s + 2 < T and (mt is None or len(tiles) < mt):
                tiles.append((s, min(NT, T - s)))
                s += NT - 2

            def prep3(i):
                s, n = tiles[i]
                xt, xb = load_x(xsB_v, "xsB", s, n, xp_rot)
                sq, sqb = prenorm_a(xt, xb, n)
                return dict(s=s, n=n, xt=xt, xb=xb, sq=sq, sqb=sqb, ybs=[])

            def prep3_b(st):
                st['h'], st['hb'] = prenorm_b(st['xt'], st['xb'], st['sq'], st['sqb'], st['n'], SC_GAIN + (0 * 2 + 1) * 8)
                return st

            def chunks3(st, c0, c1, tick):
                n, h, hb = st['n'], st['h'], st['hb']

                def proj3(bk, half, off):
                    for c in range(8):
                        MM(psum[bk][:, half * 256:half * 256 + n], WA3[:, W3IN + c * 3 * D + off: W3IN + c * 3 * D + off + 128],
                           h[:, c, 0:n], c == 0, c == 7, [hb, w3in_b[c]], [pbufs[bk]])

                for c in range(c0, c1):
                    bk = palloc()
                    proj3(bk, 0, 2 * D + c * 128)
                    proj3(bk, 1, D + c * 128)
                    usb, usbb = usb_rot.next()
                    ACT(usb[:, 0:n], psum[bk][:, 0:n], AF.Copy, [pbufs[bk]], [usbb])
                    cu, cub = cu_rot.next()
                    TT("dve", cu[:, 0:n], psum[bk][:, 256:256 + n], usb[:, 0:n], ALU.mult, [pbufs[bk], usbb], [cub])
                    pfree(bk)
                    tick()
                    bk = palloc()
                    proj3(bk, 0, c * 128)
                    wcol = SC_ODC + c * 3
                    y0, y0b = yy3_rot.next()
                    ACT(y0[:, 2:n], cu[:, 0:n - 2], AF.Identity, [cub, b_small], [y0b], scale=sc(wcol + 0))
                    y1, y1b = yy3_rot.next()
                    STT(y1[:, 2:n], cu[:, 1:n - 1], sc(wcol + 1), y0[:, 2:n], ALU.mult, ALU.add, [cub, y0b, b_small], [y1b])
                    y2, y2b = yy3_rot.next()
                    STT(y2[:, 2:n], cu[:, 2:n], sc(wcol + 2), y1[:, 2:n], ALU.mult, ALU.add, [cub, y1b, b_small], [y2b])
                    yb_, ybb = yb_rot.next()
                    TT("dve", yb_[:, 2:n], psum[bk][:, 2:n], y2[:, 2:n], ALU.mult, [pbufs[bk], y2b], [ybb])
                    pfree(bk)
                    st['ybs'].append((yb_, ybb))
                    tick()

            def p3_bslices(i, st):
                s, n, ybs = st['s'], st['n'], st['ybs']
                st['xr'] = load_x(xsB_v, "xsB", s, n, xr_rot)
                kch = [(ybs[k][0][:, 2:n], ybs[k][1], w3out_b[k]) for k in range(8)]
                banks, sl = out_proj_slices(kch, lambda k, m: WA3[:, W3OUT + k * D + m * 128: W3OUT + k * D + (m + 1) * 128], n, 2)
                st['banks'] = banks
                return sl

            def p3_bfinal_a(i, st):
                st['nsq'] = norm_residual_a(st['banks'], st['n'], 2)

            def p3_bfinal_b(i, st):
                s, n = st['s'], st['n']
                xt, xb = st['xr']
                sq, sqb = st['nsq']
                norm_residual_b(st['banks'], sq, sqb, xt, xb, n, 2, SC_GAIN + (1 * 2 + 1) * 8, s)
                store_x(xt, xb, xsA_v, "xsA", s, n, s + 2)

            def p3_A2(i, st, mid):
                mid()
                chunks3(st, 4, 8, lambda: None)

            run_pipeline3(len(tiles), prep3, prep3_b, lambda i, st, tick: chunks3(st, 0, 4, tick), p3_A2,
                          p3_bslices, p3_bfinal_a, p3_bfinal_b, prep_tick=3)
            ph3.close()
            set_fence()

        if nphase >= 4:
            ffn_phase(1, xsA_v, "xsA", outT_v, None, HALO - 2, HALO)

        S.emit(nc)
    return nc


def _chunk_cols(v):
    return np.ascontiguousarray(v.reshape(-1, 128).T)


def _rot_cols(w):
    r = w.copy()
    nh = w.shape[1] // 64
    for hd in range(nh):
        r[:, hd * 64:hd * 64 + 8] = w[:, hd * 64 + 8:hd * 64 + 16]
        r[:, hd * 64 + 8:hd * 64 + 16] = w[:, hd * 64:hd * 64 + 8]
    return r


_NC_CACHE = {}


def kernel(x, positions, mix_norm_pre, mix_norm_post, ffn_norm_pre, ffn_norm_post,
           ev_w_in, ev_a_conv_w, ev_a_conv_b, ev_a_ln_g, ev_a_ln_b, ev_sinks, ev_w_out,
           od_w_in, od_conv_w, od_w_out, ffn_w_up, ffn_conv_w, ffn_w_down):
    f32 = np.float32
    x = np.asarray(x, f32)
    positions = np.asarray(positions, np.int32)
    w_in = np.asarray(ev_w_in, f32)[0]
    q = w_in[:, 1024:1536]
    k = w_in[:, 1536:1664]
    v = w_in[:, 1664:1792]
    k0, k1 = k[:, 0:64], k[:, 64:128]
    k0r, k1r = _rot_cols(k0), _rot_cols(k1)
    w1in = np.ascontiguousarray(np.concatenate(
        [w_in[:, 0:1024], q, _rot_cols(q), k0, k0, k1, k1, k0r, k0r, k1r, k1r, v], axis=1))
    assert w1in.shape == (D, WIN_EXT)
    w1out = np.ascontiguousarray(np.asarray(ev_w_out, f32)[0])
    w3in = np.ascontiguousarray(np.asarray(od_w_in, f32)[0])
    w3out = np.ascontiguousarray(np.asarray(od_w_out, f32)[0])
    wup = np.ascontiguousarray(np.asarray(ffn_w_up, f32))
    wdn = np.ascontiguousarray(np.asarray(ffn_w_down, f32))
    small = np.zeros((128, SC_N), f32)
    gains = [mix_norm_pre, mix_norm_post, ffn_norm_pre, ffn_norm_post]
    for kind in range(4):
        for layer in range(2):
            c0 = SC_GAIN + (kind * 2 + layer) * 8
            small[:, c0:c0 + 8] = _chunk_cols(np.asarray(gains[kind], f32)[layer])
    c31 = np.asarray(ev_a_conv_w, f32)[0]
    for c in range(4):
        small[:, SC_C31 + c * 31: SC_C31 + (c + 1) * 31] = c31[:, c * 128:(c + 1) * 128].T
    small[:, SC_CB:SC_CB + 4] = _chunk_cols(np.asarray(ev_a_conv_b, f32)[0])
    small[:, SC_LNG:SC_LNG + 4] = _chunk_cols(np.asarray(ev_a_ln_g, f32)[0])
    small[:, SC_LNB:SC_LNB + 4] = _chunk_cols(np.asarray(ev_a_ln_b, f32)[0])
    sinks = np.asarray(ev_sinks, f32)[0]
    for cc in range(4):
        small[0:64, SC_SINK + cc] = sinks[2 * cc]
        small[64:128, SC_SINK + cc] = sinks[2 * cc + 1]
    odc = np.asarray(od_conv_w, f32)[0]
    for c in range(8):
        small[:, SC_ODC + c * 3: SC_ODC + c * 3 + 3] = odc[:, c * 128:(c + 1) * 128].T
    ffc = np.asarray(ffn_conv_w, f32)
    for layer in range(2):
        for c in range(NFF):
            c0 = SC_FFC + (layer * NFF + c) * 3
            small[:, c0:c0 + 3] = ffc[layer][:, c * 128:(c + 1) * 128].T
    inv_freq = (f32(500000.0) ** (-(np.arange(8, dtype=f32) * f32(2.0) / f32(16.0)))).astype(f32)
    for p in range(128):
        d = p % 64
        if d < 8:
            small[p, SC_FS] = -inv_freq[d]
        elif d < 16:
            small[p, SC_FS] = inv_freq[d - 8]
    small[:, SC_EPS_RMS] = 1e-6
    small[:, SC_EPS_LN] = 1e-5
    small[:, SC_HALFPI] = np.pi / 2
    kk = np.arange(128)[:, None]
    qq = np.arange(128)[None, :]
    m_d = np.where(kk <= qq, 0.0, NEG).astype(f32)
    m_p = np.where(kk > qq, 0.0, NEG).astype(f32)
    m_all = np.full((128, 128), NEG, f32)
    in_maps = []
    for core in range(NCORES):
        b, j = core // 4, core % 4
        start = j * TOK
        xT = np.zeros((D, T), f32)
        pos = np.zeros((T,), np.int32)
        lo = start - HALO
        if lo >= 0:
            xT[:] = x[b, lo:start + TOK, :].T
            pos[:] = positions[b, lo:start + TOK]
        else:
            xT[:, HALO:] = x[b, 0:TOK, :].T
            pos[HALO:] = positions[b, 0:TOK]
        sm = small.copy()
        sm[:, SC_HM] = 0.0 if j == 0 else 1.0
        m_f = m_all if j == 0 else m_p
        cst = np.concatenate([np.eye(128, dtype=f32), m_p, m_p, m_d, m_d, m_f, m_f, m_d, m_d], axis=1)
        in_maps.append({
            "xT": np.ascontiguousarray(xT),
            "posr": np.ascontiguousarray(np.broadcast_to(pos[None, :], (128, T))),
            "small": sm,
            "cst": np.ascontiguousarray(cst),
            "w1in": w1in, "w1out": w1out, "w3in": w3in, "w3out": w3out, "wup": wup, "wdn": wdn,
        })
    if _NC_CACHE.get("prep_only"):
        return in_maps
    if "nc" not in _NC_CACHE:
        _NC_CACHE["nc"] = build_program()
    res = run_bass_kernel_spmd(_NC_CACHE["nc"], in_maps, core_ids=list(range(NCORES)))
    out = np.empty((2, SEQ, D), f32)
    for core in range(NCORES):
        b, j = core // 4, core % 4
        out[b, j * TOK:(j + 1) * TOK, :] = res.results[core]["outT"].T
    return out
```
